# Optimizing a Trainium2 kernel written in Bass

```python
import math, functools
import jax, jax.numpy as jnp
from jax import lax
import numpy as np

D_MODEL = 2048
BATCH = 2
SEQ = 8192
DEPTH = 1
DEC_BATCH = 16
DEC_SEQ = 16
PAST_LEN = 1024

CHUNK = 64
D_FF = 4096
D_CONV = 1024
CONV_W = 3
N_HEADS = 16
N_KV = 4
HEAD_DIM = 64
GROUP = N_HEADS // N_KV
WINDOW = 128
WIN_ROWS = WINDOW
N_BAND = WINDOW // CHUNK + 1
N_BUCKETS = 32
MAX_DIST = 128
D_PLE = 256
EPS = 1e-6
NEG = -1e30

COL_SIZES = (D_CONV, D_CONV, D_CONV, N_HEADS * HEAD_DIM, N_KV * HEAD_DIM, N_KV * HEAD_DIM, D_MODEL, D_MODEL)
W_IN_COLS = sum(COL_SIZES)
SPLIT_IDX = tuple(int(s) for s in np.cumsum(COL_SIZES)[:-1])

kernel_name = "hybrid_streaming_conv_swa_step"


def rms_norm(x, g):
    xf = x.astype(jnp.float32)
    y = xf * lax.rsqrt(jnp.mean(xf * xf, axis=-1, keepdims=True) + EPS)
    return (y * g.astype(jnp.float32)).astype(x.dtype)


def swiglu(x, wg, wu, wd):
    return (jax.nn.silu(x @ wg) * (x @ wu)) @ wd


def t5_bucket(rel):
    nb = N_BUCKETS // 2
    max_exact = nb // 2
    ret = jnp.where(rel > 0, nb, 0)
    n = jnp.abs(rel)
    nf = jnp.maximum(n, 1).astype(jnp.float32)
    large = max_exact + (jnp.log(nf / max_exact) / math.log(MAX_DIST / max_exact) * (nb - max_exact)).astype(jnp.int32)
    large = jnp.minimum(large, nb - 1)
    return ret + jnp.where(n < max_exact, n, large)


def rel_bias(table, n_q, n_k):
    i = jnp.arange(n_q)[:, None]
    j = jnp.arange(n_k)[None, :]
    b = table[t5_bucket(j - WIN_ROWS - i)]
    return b.transpose(2, 0, 1).reshape(N_KV, GROUP, n_q, n_k).astype(jnp.float32)


def sink_softmax(logits, sink):
    s = sink.astype(jnp.float32).reshape(N_KV, GROUP, 1, 1)
    m = jnp.maximum(jnp.max(logits, axis=-1, keepdims=True), s)
    e = jnp.exp(logits - m)
    return e / (jnp.sum(e, axis=-1, keepdims=True) + jnp.exp(s - m))


def short_conv(u, prev, w):
    T = u.shape[1]
    full = jnp.concatenate([prev.astype(u.dtype), u], axis=1)
    out = w[0] * full[:, 0:T]
    for t in range(1, CONV_W):
        out = out + w[t] * full[:, t:t + T]
    return out, full[:, -(CONV_W - 1):]


def swa_prompt(q, k, v, bias, sink):
    B, T = q.shape[:2]
    nc = T // CHUNK
    scale = HEAD_DIM ** -0.5
    qb = q.reshape(B, nc, CHUNK, N_KV, GROUP, HEAD_DIM)
    pad = ((0, 0), (WINDOW, 0), (0, 0), (0, 0))
    kp = jnp.pad(k, pad).reshape(B, nc + N_BAND - 1, CHUNK, N_KV, HEAD_DIM)
    vp = jnp.pad(v, pad).reshape(B, nc + N_BAND - 1, CHUNK, N_KV, HEAD_DIM)
    kb = jnp.concatenate([kp[:, i:i + nc] for i in range(N_BAND)], axis=2)
    vb = jnp.concatenate([vp[:, i:i + nc] for i in range(N_BAND)], axis=2)
    logits = jnp.einsum('bcqkgd,bcskd->bckgqs', qb, kb).astype(jnp.float32) * scale + bias
    key_pos = jnp.arange(nc)[:, None] * CHUNK + jnp.arange(N_BAND * CHUNK)[None, :] - WINDOW
    logits = jnp.where((key_pos >= 0)[None, :, None, None, None, :], logits, NEG)
    p = sink_softmax(logits, sink).astype(v.dtype)
    o = jnp.einsum('bckgqs,bcskd->bcqkgd', p, vb).reshape(B, T, N_HEADS * HEAD_DIM)
    return o, k[:, -WIN_ROWS:], v[:, -WIN_ROWS:]


def swa_sample(q, k, v, k_cache, v_cache, bias, sink):
    B, S = q.shape[:2]
    scale = HEAD_DIM ** -0.5
    qh = q.reshape(B, S, N_KV, GROUP, HEAD_DIM)
    ka = jnp.concatenate([k_cache.astype(k.dtype), k], axis=1)
    va = jnp.concatenate([v_cache.astype(v.dtype), v], axis=1)
    logits = jnp.einsum('bqkgd,bskd->bkgqs', qh, ka).astype(jnp.float32) * scale + bias
    p = sink_softmax(logits, sink).astype(v.dtype)
    o = jnp.einsum('bkgqs,bskd->bqkgd', p, va).reshape(B, S, N_HEADS * HEAD_DIM)
    return o, k, v


def layer(x, pe, conv_prev, attend, w):
    (f1_norm, f1_wg, f1_wu, f1_wd, mix_norm, w_in, conv_w, q_norm, k_norm,
     w_conv_out, w_attn_o, w_out, f2_norm, f2_wg, f2_wu, f2_wd, ple_norm, w_ple, w_ple_gate) = w
    B, T = x.shape[:2]
    h = x + 0.5 * swiglu(rms_norm(x, f1_norm), f1_wg, f1_wu, f1_wd)
    n = rms_norm(h, mix_norm)
    cb, cc, cv, q, k, v, gc, ga = jnp.split(n @ w_in, SPLIT_IDX, axis=-1)
    cu, conv_state = short_conv(cc * cv, conv_prev, conv_w)
    y_conv = (cb * cu) @ w_conv_out
    q = rms_norm(q.reshape(B, T, N_HEADS, HEAD_DIM), q_norm)
    k = rms_norm(k.reshape(B, T, N_KV, HEAD_DIM), k_norm)
    v = v.reshape(B, T, N_KV, HEAD_DIM)
    o, k_state, v_state = attend(q, k, v)
    y_attn = o @ w_attn_o
    h = h + (jax.nn.sigmoid(gc) * y_conv + jax.nn.sigmoid(ga) * y_attn) @ w_out
    h = h + 0.5 * swiglu(rms_norm(h, f2_norm), f2_wg, f2_wu, f2_wd)
    h = h + (pe @ w_ple) * jax.nn.sigmoid(rms_norm(h, ple_norm) @ w_ple_gate)
    return h, conv_state, k_state, v_state


def setup_inputs(seed: int = 0) -> dict:
    key = jax.random.key(seed)
    ks = iter(jax.random.split(key, 40))
    f32 = jnp.float32

    def nrm(shape, scale):
        return jax.random.normal(next(ks), shape, f32) * scale

    def gain(shape):
        return 1.0 + nrm(shape, 0.02)

    L, D = DEPTH, D_MODEL
    return {
        "x_prompt": nrm((BATCH, SEQ, D), 1.0),
        "x_sample": nrm((DEC_BATCH, DEC_SEQ, D), 1.0),
        "p_prompt": nrm((DEPTH, BATCH, SEQ, D_PLE), 1.0),
        "p_sample": nrm((DEPTH, DEC_BATCH, DEC_SEQ, D_PLE), 1.0),
        "state_conv": nrm((DEPTH, DEC_BATCH, CONV_W - 1, D_CONV), 0.5),
        "cache_k": nrm((DEPTH, DEC_BATCH, WIN_ROWS, N_KV, HEAD_DIM), 1.0),
        "cache_v": nrm((DEPTH, DEC_BATCH, WIN_ROWS, N_KV, HEAD_DIM), 1.0),
        "rel_table": nrm((N_BUCKETS, N_HEADS), 0.1),
        "ffn1_norm": gain((L, D)),
        "ffn1_wg": nrm((L, D, D_FF), D ** -0.5),
        "ffn1_wu": nrm((L, D, D_FF), D ** -0.5),
        "ffn1_wd": nrm((L, D_FF, D), D_FF ** -0.5),
        "mix_norm": gain((L, D)),
        "w_in": nrm((L, D, W_IN_COLS), D ** -0.5),
        "conv_w": nrm((L, CONV_W, D_CONV), CONV_W ** -0.5),
        "q_norm": gain((L, HEAD_DIM)),
        "k_norm": gain((L, HEAD_DIM)),
        "attn_sink": nrm((L, N_HEADS), 0.5),
        "w_conv_out": nrm((L, D_CONV, D), D_CONV ** -0.5),
        "w_attn_o": nrm((L, N_HEADS * HEAD_DIM, D), (N_HEADS * HEAD_DIM) ** -0.5),
        "w_out": nrm((L, D, D), D ** -0.5),
        "ffn2_norm": gain((L, D)),
        "ffn2_wg": nrm((L, D, D_FF), D ** -0.5),
        "ffn2_wu": nrm((L, D, D_FF), D ** -0.5),
        "ffn2_wd": nrm((L, D_FF, D), D_FF ** -0.5),
        "ple_norm": gain((L, D)),
        "w_ple": nrm((L, D_PLE, D), D_PLE ** -0.5),
        "w_ple_gate": nrm((L, D, D), D ** -0.5),
    }


def reference(x_prompt, x_sample, p_prompt, p_sample, state_conv, cache_k, cache_v, rel_table,
              ffn1_norm, ffn1_wg, ffn1_wu, ffn1_wd, mix_norm, w_in, conv_w, q_norm, k_norm, attn_sink,
              w_conv_out, w_attn_o, w_out, ffn2_norm, ffn2_wg, ffn2_wu, ffn2_wd, ple_norm, w_ple, w_ple_gate):
    S = x_sample.shape[1]
    bias_p = rel_bias(rel_table, CHUNK, WIN_ROWS + CHUNK)
    bias_s = rel_bias(rel_table, S, WIN_ROWS + S)
    yp, ys = x_prompt, x_sample
    conv_p, k_p, v_p, conv_s, k_s, v_s = [], [], [], [], [], []
    for l in range(DEPTH):
        w = (ffn1_norm[l], ffn1_wg[l], ffn1_wu[l], ffn1_wd[l], mix_norm[l], w_in[l], conv_w[l],
             q_norm[l], k_norm[l], w_conv_out[l], w_attn_o[l], w_out[l],
             ffn2_norm[l], ffn2_wg[l], ffn2_wu[l], ffn2_wd[l], ple_norm[l], w_ple[l], w_ple_gate[l])
        zero_prev = jnp.zeros((yp.shape[0], CONV_W - 1, D_CONV), yp.dtype)
        attend_p = functools.partial(swa_prompt, bias=bias_p, sink=attn_sink[l])
        yp, cs, kk, vv = layer(yp, p_prompt[l], zero_prev, attend_p, w)
        conv_p.append(cs); k_p.append(kk); v_p.append(vv)
        attend_s = functools.partial(swa_sample, k_cache=cache_k[l], v_cache=cache_v[l], bias=bias_s, sink=attn_sink[l])
        ys, cs, kk, vv = layer(ys, p_sample[l], state_conv[l], attend_s, w)
        conv_s.append(cs); k_s.append(kk); v_s.append(vv)
    conv_prompt = jnp.stack(conv_p)
    k_prompt = jnp.stack(k_p)
    v_prompt = jnp.stack(v_p)
    conv_sample = jnp.stack(conv_s)
    k_sample = jnp.stack(k_s)
    v_sample = jnp.stack(v_s)
    return (yp, ys, conv_prompt, k_prompt, v_prompt, conv_sample, k_sample, v_sample)
```

```python
import numpy as np
from contextlib import ExitStack
import concourse.bass as bass
import concourse.mybir as mybir
from concourse.bass_utils import run_bass_kernel_spmd

F32 = mybir.dt.float32
BF16 = mybir.dt.bfloat16
AF = mybir.ActivationFunctionType
ALU = mybir.AluOpType

D = 2048
DFF = 4096
DC = 1024
NH = 16
NKV = 4
HD = 64
DPLE = 256
WIN_COLS = 8704
EPS = 1e-6
NCORES = 8
TP = 512
NT = 4
HALO = 128
SS = 32
KC = D // 128

O_CB, O_CC, O_CV, O_Q, O_K, O_V, O_GC, O_GA = 0, 1024, 2048, 3072, 4096, 4352, 4608, 6656


WGEOM = {
    "ffn1_wg": (D, DFF, 128, 16), "ffn1_wu": (D, DFF, 128, 16), "ffn1_wd": (DFF, D, 128, 16),
    "w_in": (D, WIN_COLS, 128, 16), "w_conv_out": (DC, D, 128, 8), "w_attn_o": (NH * HD, D, 64, 16),
    "w_out": (D, D, 128, 16), "ffn2_wg": (D, DFF, 128, 16), "ffn2_wu": (D, DFF, 128, 16),
    "ffn2_wd": (DFF, D, 128, 16), "w_ple": (DPLE, D, 128, 2), "w_ple_gate": (D, D, 128, 16),
}


def panelize(w, pk, nk):
    R, C = w.shape
    rb, cb = R // (nk * pk), C // 256
    return np.ascontiguousarray(w.reshape(rb, nk, pk, cb, 256).transpose(0, 3, 2, 1, 4))


class Op:
    __slots__ = ("eng", "fn", "deps", "needs_inc", "sem", "val", "is_dma", "prev", "name")


class Prog:
    ENGS = ("pe", "act", "dve", "pool", "sp")
    NRING = 8

    def __init__(self, nc, es):
        self.nc = nc
        self.streams = {e: [] for e in self.ENGS}
        self.lastw = {}
        self.readers = {}
        self.esem = {e: es.enter_context(nc.semaphore("s_" + e)) for e in ("pe", "act", "dve", "pool")}
        self.ring = {q: [es.enter_context(nc.semaphore("d_%s_%d" % (q, i))) for i in range(self.NRING)]
                     for q in ("sp", "pool")}
        self.ring_n = {q: 0 for q in ("sp", "pool")}
        self.ring_cnt = {}
        self.out_ops = []

    def op(self, eng, fn, reads=(), writes=(), dma=False, name=""):
        o = Op()
        o.eng, o.fn, o.is_dma, o.needs_inc, o.name = eng, fn, dma, False, name
        o.sem = None
        o.val = 0
        o.prev = None
        deps = set()
        for r in reads:
            w = self.lastw.get(r)
            if w is not None:
                deps.add(w)
        for r in writes:
            w = self.lastw.get(r)
            if w is not None:
                deps.add(w)
            for x in self.readers.get(r, ()):
                deps.add(x)
        deps.discard(o)
        o.deps = deps
        for d in deps:
            if not (d.eng == "pe" and eng == "pe" and not d.is_dma):
                d.needs_inc = True
        for r in reads:
            self.readers.setdefault(r, []).append(o)
        for r in writes:
            self.lastw[r] = o
            self.readers[r] = []
        if dma:
            q = eng
            i = self.ring_n[q] % self.NRING
            self.ring_n[q] += 1
            sem = self.ring[q][i]
            k = self.ring_cnt.get((q, i), 0)
            self.ring_cnt[(q, i)] = k + 1
            o.sem = sem
            o.val = 16 * (k + 1)
            if k > 0:
                o.prev = (sem, 16 * k)
        self.streams[eng].append(o)
        return o

    def emit(self):
        nc = self.nc
        for e in ("pe", "act", "dve", "pool"):
            cnt = 0
            for o in self.streams[e]:
                if o.is_dma:
                    continue
                if o.needs_inc:
                    cnt += 1
                    o.sem = self.esem[e]
                    o.val = cnt
        fin = Op()
        fin.eng, fin.fn, fin.is_dma, fin.needs_inc, fin.name = "sp", None, False, False, "final"
        fin.deps = set(self.out_ops)
        lastd = {}
        for q in ("sp", "pool"):
            for o in self.streams[q]:
                if o.is_dma:
                    lastd[id(o.sem)] = o
        fin.deps.update(lastd.values())
        fin.prev = None
        self.streams["sp"].append(fin)

        def run_stream(e, h):
            seen = {}
            for o in self.streams[e]:
                need = {}
                for d in o.deps:
                    if d.eng == "pe" and e == "pe" and not d.is_dma:
                        continue
                    if d.sem is None:
                        raise RuntimeError("dep without sem: %s -> %s" % (d.name, o.name))
                    key = id(d.sem)
                    if key not in need or need[key][1] < d.val:
                        need[key] = (d.sem, d.val)
                if o.prev is not None:
                    key = id(o.prev[0])
                    if key not in need or need[key][1] < o.prev[1]:
                        need[key] = o.prev
                for key, (sem, val) in need.items():
                    if seen.get(key, 0) >= val:
                        continue
                    h.wait_ge(sem, val)
                    seen[key] = val
                if o.fn is None:
                    continue
                ins = o.fn(h)
                if o.is_dma:
                    ins.then_inc(o.sem, 16)
                elif o.needs_inc:
                    ins.then_inc(o.sem, 1)

        with nc.Block() as block:
            @block.tensor
            def _(h):
                run_stream("pe", h)

            @block.scalar
            def _(h):
                run_stream("act", h)

            @block.vector
            def _(h):
                run_stream("dve", h)

            @block.gpsimd
            def _(h):
                run_stream("pool", h)

            @block.sync
            def _(h):
                run_stream("sp", h)


class K:
    pass


def t5_bucket_np(rel):
    nb = 16
    max_exact = 8
    ret = np.where(rel > 0, nb, 0)
    nabs = np.abs(rel)
    nf = np.maximum(nabs, 1).astype(np.float32)
    large = max_exact + (np.log(nf / max_exact) / np.float32(np.log(128 / max_exact)) * (nb - max_exact)).astype(np.int32)
    large = np.minimum(large, nb - 1)
    return ret + np.where(nabs < max_exact, nabs, large)


def onehot_const():
    r = np.arange(255) - 63 - 128
    b = t5_bucket_np(r)
    oh = np.zeros((32, 256), np.float32)
    oh[b, np.arange(255)] = 1.0
    return oh


class _Stop(Exception):
    pass


def build(ntiles=5, do_sample=True, stop=None):
    nc = bass.Bass("TRN2", target_bir_lowering=False)
    es = ExitStack()
    with es:
        P = Prog(nc, es)

        def dram_in(name, shape, dt=F32):
            return nc.dram_tensor(name, list(shape), dt, kind="ExternalInput").ap()

        def dram_out(name, shape, dt=F32):
            return nc.dram_tensor(name, list(shape), dt, kind="ExternalOutput").ap()

        def sb(name, shape, dt):
            return es.enter_context(nc.sbuf_tensor("sb_" + name, list(shape), dt))

        xT = dram_in("xT", [D, HALO + NT * TP])
        xsT = dram_in("xsT", [D, SS])
        peT = dram_in("peT", [DPLE, NT * TP])
        pesT = dram_in("pesT", [DPLE, SS])
        uprev_s = dram_in("uprev_s", [128, 8, 2, 2])
        kcT_d = dram_in("kcT", [64, 2, NKV, 128])
        vc_d = dram_in("vc", [64, 2, 2, 256])
        rel_table = dram_in("rel_table", [32, 16])
        oh_d = dram_in("onehot", [32, 256])
        hmask_d = dram_in("hmask", [64, 1])
        norms_d = dram_in("norms", [128, 4, KC])
        convw_d = dram_in("convw", [128, 8, 3])
        qkn_d = dram_in("qkn", [64, 2])
        sink_d = dram_in("sink", [1, 16])
        ffn1_wg = dram_in("ffn1_wg", [WGEOM["ffn1_wg"][0] // (WGEOM["ffn1_wg"][2] * WGEOM["ffn1_wg"][3]), WGEOM["ffn1_wg"][1] // 256, WGEOM["ffn1_wg"][2], WGEOM["ffn1_wg"][3], 256])
        ffn1_wu = dram_in("ffn1_wu", [WGEOM["ffn1_wu"][0] // (WGEOM["ffn1_wu"][2] * WGEOM["ffn1_wu"][3]), WGEOM["ffn1_wu"][1] // 256, WGEOM["ffn1_wu"][2], WGEOM["ffn1_wu"][3], 256])
        ffn1_wd = dram_in("ffn1_wd", [WGEOM["ffn1_wd"][0] // (WGEOM["ffn1_wd"][2] * WGEOM["ffn1_wd"][3]), WGEOM["ffn1_wd"][1] // 256, WGEOM["ffn1_wd"][2], WGEOM["ffn1_wd"][3], 256])
        w_in = dram_in("w_in", [WGEOM["w_in"][0] // (WGEOM["w_in"][2] * WGEOM["w_in"][3]), WGEOM["w_in"][1] // 256, WGEOM["w_in"][2], WGEOM["w_in"][3], 256])
        w_conv_out = dram_in("w_conv_out", [WGEOM["w_conv_out"][0] // (WGEOM["w_conv_out"][2] * WGEOM["w_conv_out"][3]), WGEOM["w_conv_out"][1] // 256, WGEOM["w_conv_out"][2], WGEOM["w_conv_out"][3], 256])
        w_attn_o = dram_in("w_attn_o", [WGEOM["w_attn_o"][0] // (WGEOM["w_attn_o"][2] * WGEOM["w_attn_o"][3]), WGEOM["w_attn_o"][1] // 256, WGEOM["w_attn_o"][2], WGEOM["w_attn_o"][3], 256])
        w_out = dram_in("w_out", [WGEOM["w_out"][0] // (WGEOM["w_out"][2] * WGEOM["w_out"][3]), WGEOM["w_out"][1] // 256, WGEOM["w_out"][2], WGEOM["w_out"][3], 256])
        ffn2_wg = dram_in("ffn2_wg", [WGEOM["ffn2_wg"][0] // (WGEOM["ffn2_wg"][2] * WGEOM["ffn2_wg"][3]), WGEOM["ffn2_wg"][1] // 256, WGEOM["ffn2_wg"][2], WGEOM["ffn2_wg"][3], 256])
        ffn2_wu = dram_in("ffn2_wu", [WGEOM["ffn2_wu"][0] // (WGEOM["ffn2_wu"][2] * WGEOM["ffn2_wu"][3]), WGEOM["ffn2_wu"][1] // 256, WGEOM["ffn2_wu"][2], WGEOM["ffn2_wu"][3], 256])
        ffn2_wd = dram_in("ffn2_wd", [WGEOM["ffn2_wd"][0] // (WGEOM["ffn2_wd"][2] * WGEOM["ffn2_wd"][3]), WGEOM["ffn2_wd"][1] // 256, WGEOM["ffn2_wd"][2], WGEOM["ffn2_wd"][3], 256])
        w_ple = dram_in("w_ple", [WGEOM["w_ple"][0] // (WGEOM["w_ple"][2] * WGEOM["w_ple"][3]), WGEOM["w_ple"][1] // 256, WGEOM["w_ple"][2], WGEOM["w_ple"][3], 256])
        w_ple_gate = dram_in("w_ple_gate", [WGEOM["w_ple_gate"][0] // (WGEOM["w_ple_gate"][2] * WGEOM["w_ple_gate"][3]), WGEOM["w_ple_gate"][1] // 256, WGEOM["w_ple_gate"][2], WGEOM["w_ple_gate"][3], 256])

        SCR_TOTAL = 2 * 3 * D * DFF + D * WIN_COLS + 2 * DC * D + 2 * D * D + DPLE * D
        scr = nc.dram_tensor("wscr", [SCR_TOTAL], BF16, kind="ExternalOutput").ap()
        wnames = {}
        for _nm, _ap in (("ffn1_wg", ffn1_wg), ("ffn1_wu", ffn1_wu), ("ffn1_wd", ffn1_wd), ("w_in", w_in),
                         ("w_conv_out", w_conv_out), ("w_attn_o", w_attn_o), ("w_out", w_out), ("ffn2_wg", ffn2_wg),
                         ("ffn2_wu", ffn2_wu), ("ffn2_wd", ffn2_wd), ("w_ple", w_ple), ("w_ple_gate", w_ple_gate)):
            wnames[id(_ap)] = _nm
        yT = dram_out("yT", [D, NT * TP])
        ysT = dram_out("ysT", [D, SS])
        convp_o = dram_out("convp", [128, 8, 2])
        kp_o = dram_out("kp", [64, NKV, 128])
        vp_o = dram_out("vp", [64, 2, 256])
        convs_o = dram_out("convs", [128, 8, 2, 2])
        ks_o = dram_out("ks", [64, NKV, SS])
        vs_o = dram_out("vs", [32, 256])

        TMAX = TP
        h = sb("h", [128, KC, TMAX], F32)
        n = sb("n", [128, KC, TMAX], BF16)
        act = sb("act", [128, 32, TMAX], BF16)
        z = sb("z", [128, 8, TMAX], BF16)
        NSLOT = 4
        wslots = [sb("w%d" % i, [128, 16, 256], BF16) for i in range(NSLOT)]
        ones_bf = sb("ones_bf", [128, 128], BF16)
        norms = sb("norms", [128, 4, KC], F32)
        epsc = sb("epsc", [128, 1], F32)
        rstd = sb("rstd", [128, TMAX], F32)
        sq = [sb("sq%d" % i, [128, TMAX], BF16) for i in range(2)]
        sg = [sb("sg%d" % i, [128, TMAX], F32) for i in range(4)]
        ub = [sb("ub%d" % i, [128, TMAX + 2], F32) for i in range(2)]
        uhist = sb("uhist", [128, 8, 2], F32)
        uhist_s = sb("uhist_s", [128, 8, 2, 2], F32)
        convw = sb("convw", [128, 8, 3], F32)
        kT = sb("kT", [64, NKV, HALO + TMAX], BF16)
        kT32 = sb("kT32", [64, NKV, 160], F32)
        V = sb("V", [64, 10, 256], BF16)
        V32 = sb("V32", [64, 3, 256], F32)
        kcT = sb("kcT", [64, 2, NKV, 128], BF16)
        vcs = sb("vcs", [64, 2, 2, 256], BF16)
        biasT = [sb("biasT%d" % kv, [64, 3, 4, 64], F32) for kv in range(NKV)]
        biasS = [sb("biasS%d" % kv, [32, 2, 4, 16], F32) for kv in range(NKV)]
        oh = sb("oh", [32, 256], F32)
        tab = sb("tab", [32, 16], F32)
        hmask = sb("hmask", [64, 1], F32)
        qkn = sb("qkn", [64, 2], F32)
        sink = sb("sink", [1, 16], F32)
        sinkrow = sb("sinkrow", [1, 16, 64], BF16)
        lt = [sb("lt%d" % i, [64, 3, 256], F32) for i in range(2)]
        pT = [sb("pT%d" % i, [64, 3, 256], BF16) for i in range(2)]
        rec = [sb("rec%d" % i, [64, 256], F32) for i in range(2)]
        pe_bf = sb("pe_bf", [128, 2, TMAX], BF16)
        ps = [es.enter_context(nc.psum_tensor("ps%d" % i, [128, 1024], F32)) for i in range(4)]

        def bank(i):
            return ps[i // 2][:, (i % 2) * 512:(i % 2) * 512 + 512]

        st = K()
        st.wslot_n = 0
        st.scr_off = 0
        st.scr_cache = {}
        st.cache_on = True
        st.tile_idx = 0
        SAVE_TILE = {"w_conv_out": 1, "w_attn_o": 1, "w_out": 1, "w_ple": 1, "w_ple_gate": 1,
                     "ffn2_wg": 2, "ffn2_wu": 2, "ffn2_wd": 3}
        st.rr = {}

        def rr(name, nmax):
            v = st.rr.get(name, 0)
            st.rr[name] = v + 1
            return v % nmax

        def load_panel(W, r0, c0, ncols=256, nk=16, pk=128):
            i = st.wslot_n % NSLOT
            st.wslot_n += 1
            dst = wslots[i][0:pk, 0:nk, 0:ncols]
            key = (wnames[id(W)], r0, c0, ncols, nk, pk)
            ne = pk * nk * ncols
            if key in st.scr_cache:
                off = st.scr_cache[key]
                src = scr[off:off + ne].rearrange("(p k n) -> p k n", p=pk, k=nk)
                P.op("sp", lambda e, dst=dst, src=src: e.dma_start(out=dst, in_=src),
                     reads=[("scr", key)], writes=[("w", i)], dma=True, name="wload2")
            else:
                assert ncols == 256 and c0 % 256 == 0 and r0 % (nk * pk) == 0 and WGEOM[key[0]][2:] == (pk, nk)
                src = W[r0 // (nk * pk), c0 // 256]
                P.op("pool", lambda e, dst=dst, src=src: e.dma_start(out=dst, in_=src),
                     reads=(), writes=[("w", i)], dma=True, name="wload")
                if st.cache_on and st.tile_idx >= SAVE_TILE.get(key[0], 0):
                    off = st.scr_off
                    st.scr_off += ne
                    st.scr_cache[key] = off
                    sdst = scr[off:off + ne].rearrange("(p k n) -> p k n", p=pk, k=nk)
                    P.op("sp", lambda e, sdst=sdst, dst=dst: e.dma_start(out=sdst, in_=dst),
                         reads=[("w", i)], writes=[("scr", key)], dma=True, name="wsave")
            return i

        def dma(q, dst, src, reads=(), writes=(), out=False, name="dma"):
            o = P.op(q, lambda e, dst=dst, src=src: e.dma_start(out=dst, in_=src), reads=reads, writes=writes,
                     dma=True, name=name)
            if out:
                P.out_ops.append(o)
            return o

        def mm(out, lhsT, rhs, start, stop, reads, writes, name="mm"):
            P.op("pe", lambda e: e.matmul(out, lhsT, rhs, start=start, stop=stop), reads=reads, writes=writes,
                 name=name)

        def const_setup():
            P.op("dve", lambda e: e.memset(ones_bf[:], 1.0), writes=[("ones",)], name="ones")
            P.op("dve", lambda e: e.memset(epsc[:], EPS), writes=[("epsc",)], name="epsc")
            P.op("dve", lambda e: e.memset(uhist[:], 0.0), writes=[("uhist", j) for j in range(8)], name="uh0")
            P.op("dve", lambda e: e.memset(pe_bf[:], 0.0), writes=[("pe_bf",)], name="pe0")
            dma("sp", norms[:], norms_d, writes=[("norms",)])
            dma("sp", convw[:], convw_d, writes=[("convw",)])
            dma("sp", qkn[:], qkn_d, writes=[("qkn",)])
            dma("sp", sink[:], sink_d, writes=[("sink",)])
            dma("sp", oh[:], oh_d, writes=[("oh",)])
            dma("sp", tab[:], rel_table, writes=[("tab",)])
            dma("sp", hmask[:], hmask_d, writes=[("hmask",)])
            dma("sp", uhist_s[:], uprev_s, writes=[("uhist_s",)])
            dma("pool", kcT[:], kcT_d, writes=[("kcT",)])
            dma("pool", vcs[:], vc_d, writes=[("vcs",)])
            P.op("act", lambda e: e.activation(out=sink[:], in_=sink[:], func=AF.Exp),
                 reads=[("sink",)], writes=[("sink",)], name="expsink")
            P.op("dve", lambda e: e.tensor_copy(out=sinkrow[:], in_=bass.AP(sink, 0, [[16, 1], [1, 16], [0, 64]])),
                 reads=[("sink",)], writes=[("sinkrow",)], name="sinkrow")
            for kc in range(3):
                pt = ps[kc % 2]
                for q in range(64):
                    r0 = 63 - q + kc * 64
                    mm(pt[0:64, q * 16:(q + 1) * 16], oh[:, r0:r0 + 64], tab[:, :], True, True,
                       reads=[("oh",), ("tab",)], writes=[("ps", (kc % 2) * 2), ("ps", (kc % 2) * 2 + 1)], name="biasmm")
                for kv in range(NKV):
                    src = bass.AP(pt, kv * 4, [[1024, 64], [1, 4], [16, 64]])
                    P.op("dve", lambda e, src=src, kv=kv, kc=kc: e.tensor_copy(out=biasT[kv][:, kc, :, :], in_=src),
                         reads=[("ps", (kc % 2) * 2), ("ps", (kc % 2) * 2 + 1)], writes=[("biasT", kv)], name="biascp")

        def bias_sample_setup():
            for kv in range(NKV):
                P.op("dve", lambda e, kv=kv: e.memset(biasS[kv][:], -1e30), writes=[("biasS", kv)], name="bsms")
                for b in range(2):
                    dma("sp", biasS[kv][b * 16:(b + 1) * 16, b, :, :], biasT[kv][0:16, 2, :, 0:16],
                        reads=[("biasT", kv)], writes=[("biasS", kv)], name="bsdma")

        def load_x(src_ap, t0, T, c0):
            for q in range(4):
                src = src_ap[q * 512:(q + 1) * 512, t0:t0 + T].rearrange("(c p) t -> p c t", p=128)
                dst = h[:, q * 4:(q + 1) * 4, c0:c0 + T]
                dma("sp", dst, src, writes=[("h", c) for c in range(q * 4, q * 4 + 4)], name="xload")

        def store_y(dst_ap, t0, T, c0):
            for q in range(4):
                dst = dst_ap[q * 512:(q + 1) * 512, t0:t0 + T].rearrange("(c p) t -> p c t", p=128)
                src = h[:, q * 4:(q + 1) * 4, c0:c0 + T]
                dma("sp", dst, src, reads=[("h", c) for c in range(q * 4, q * 4 + 4)], out=True, name="ystore")

        def rmsnorm(T, gi):
            sb_ = bank(6)
            for c in range(KC):
                s = sq[c % 2]
                P.op("act", lambda e, s=s, c=c: e.activation(out=s[:, 0:T], in_=h[:, c, 0:T], func=AF.Square),
                     reads=[("h", c)], writes=[("sq", c % 2)], name="sq")
                mm(sb_[:, 0:T], ones_bf[:], s[:, 0:T], c == 0, c == KC - 1,
                   reads=[("sq", c % 2), ("ones",)], writes=[("ps", 6)], name="ssq")
            P.op("act", lambda e: e.activation(out=rstd[:, 0:T], in_=sb_[:, 0:T], func=AF.Ln, bias=epsc[:, 0:1],
                                               scale=1.0 / D),
                 reads=[("ps", 6), ("epsc",)], writes=[("rstd",)], name="rstd1")
            P.op("act", lambda e: e.activation(out=rstd[:, 0:T], in_=rstd[:, 0:T], func=AF.Exp, scale=-0.5),
                 reads=[("rstd",)], writes=[("rstd",)], name="rstd2")
            for c in range(KC):
                eng = "dve"
                P.op(eng, lambda e, c=c: e.scalar_tensor_tensor(out=n[:, c, 0:T], in0=h[:, c, 0:T],
                                                                scalar=norms[:, gi, c:c + 1], in1=rstd[:, 0:T],
                                                                op0=ALU.mult, op1=ALU.mult),
                     reads=[("h", c), ("rstd",), ("norms",)], writes=[("n", c)], name="nrm")

        def ffn(T, gi, Wg, Wu, Wd):
            rmsnorm(T, gi)
            NP = DFF // 256
            pend = [(load_panel(Wg, 0, 0), load_panel(Wu, 0, 0))]
            for p in range(NP):
                if p + 1 < NP:
                    pend.append((load_panel(Wg, 0, (p + 1) * 256), load_panel(Wu, 0, (p + 1) * 256)))
                sgi, sui = pend[p]
                for sub in range(2):
                    j = p * 2 + sub
                    ba_i, bb_i = (j % 2) * 2, (j % 2) * 2 + 1
                    bA, bB = bank(ba_i), bank(bb_i)
                    for k in range(KC):
                        mm(bA[:, 0:T], wslots[sgi][:, k, sub * 128:(sub + 1) * 128], n[:, k, 0:T], k == 0, k == KC - 1,
                           reads=[("w", sgi), ("n", k)], writes=[("ps", ba_i)], name="mmg")
                    for k in range(KC):
                        mm(bB[:, 0:T], wslots[sui][:, k, sub * 128:(sub + 1) * 128], n[:, k, 0:T], k == 0, k == KC - 1,
                           reads=[("w", sui), ("n", k)], writes=[("ps", bb_i)], name="mmu")
                    s = sg[j % 2]
                    P.op("act", lambda e, s=s, bA=bA: e.activation(out=s[:, 0:T], in_=bA[:, 0:T], func=AF.Silu),
                         reads=[("ps", ba_i)], writes=[("sg", j % 2)], name="silu")
                    P.op("dve", lambda e, s=s, bB=bB, j=j: e.tensor_tensor(out=act[:, j, 0:T], in0=s[:, 0:T],
                                                                             in1=bB[:, 0:T], op=ALU.mult),
                         reads=[("sg", j % 2), ("ps", bb_i)], writes=[("act", j)], name="gu")
            NPD = D // 256
            pend = [(load_panel(Wd, 0, 0), load_panel(Wd, 2048, 0))]
            for p in range(NPD):
                if p + 1 < NPD:
                    pend.append((load_panel(Wd, 0, (p + 1) * 256), load_panel(Wd, 2048, (p + 1) * 256)))
                s0, s1 = pend[p]
                for sub in range(2):
                    i = p * 2 + sub
                    bi = 4 + (i % 2)
                    b = bank(bi)
                    for k in range(32):
                        sl = s0 if k < 16 else s1
                        mm(b[:, 0:T], wslots[sl][:, k % 16, sub * 128:(sub + 1) * 128], act[:, k, 0:T], k == 0, k == 31,
                           reads=[("w", sl), ("act", k)], writes=[("ps", bi)], name="mmd")
                    P.op("dve", lambda e, b=b, i=i: e.scalar_tensor_tensor(
                        out=h[:, i, 0:T], in0=b[:, 0:T], scalar=0.5, in1=h[:, i, 0:T], op0=ALU.mult, op1=ALU.add),
                        reads=[("ps", bi), ("h", i)], writes=[("h", i)], name="resid")

        def head_A(T, panel, col, gcol, dst_bf, dst_regs, dst32=None, dst32_regs=(), c32=None):
            if panel not in st.hp_panels:
                st.hp_panels[panel] = load_panel(w_in, 0, panel)
            slot = st.hp_panels[panel]
            bi = rr("hp", 2)
            b = bank(bi)
            for k in range(KC):
                mm(b[0:64, 0:T], wslots[slot][:, k, col:col + 64], n[:, k, 0:T], k == 0, k == KC - 1,
                   reads=[("w", slot), ("n", k)], writes=[("ps", bi)], name="mmh")
            si = rr("hps", 2)
            raw = sg[si]
            P.op("act", lambda e: e.activation(out=raw[0:64, 0:T], in_=b[0:64, 0:T], func=AF.Copy),
                 reads=[("ps", bi)], writes=[("sg", si)], name="hraw")
            s2 = sq[si]
            P.op("dve", lambda e: e.tensor_tensor(out=s2[0:64, 0:T], in0=raw[0:64, 0:T], in1=raw[0:64, 0:T], op=ALU.mult),
                 reads=[("sg", si)], writes=[("sq", si)], name="hsq")
            return (T, si, gcol, dst_bf, dst_regs, dst32, dst32_regs, c32)

        def head_B(ctx):
            T, si, gcol, dst_bf, dst_regs, dst32, dst32_regs, c32 = ctx
            raw, s2 = sg[si], sq[si]
            b2i = 6 + rr("hpb", 2)
            b2 = bank(b2i)
            mm(b2[0:64, 0:T], ones_bf[0:64, 0:64], s2[0:64, 0:T], True, True,
               reads=[("sq", si), ("ones",)], writes=[("ps", b2i)], name="hss")
            r2 = sg[2 + si]
            P.op("act", lambda e: e.activation(out=r2[0:64, 0:T], in_=b2[0:64, 0:T], func=AF.Ln, bias=epsc[0:64, 0:1],
                                               scale=1.0 / HD),
                 reads=[("ps", b2i), ("epsc",)], writes=[("sg", 2 + si)], name="hrs1")
            P.op("act", lambda e: e.activation(out=r2[0:64, 0:T], in_=r2[0:64, 0:T], func=AF.Exp, scale=-0.5),
                 reads=[("sg", 2 + si)], writes=[("sg", 2 + si)], name="hrs2")
            P.op("dve", lambda e: e.scalar_tensor_tensor(out=dst_bf, in0=raw[0:64, 0:T], scalar=qkn[:, gcol:gcol + 1],
                                                         in1=r2[0:64, 0:T], op0=ALU.mult, op1=ALU.mult),
                 reads=[("sg", si), ("sg", 2 + si), ("qkn",)], writes=dst_regs, name="hnorm")
            if dst32 is not None:
                P.op("dve", lambda e: e.scalar_tensor_tensor(out=dst32, in0=raw[0:64, c32[0]:c32[1]],
                                                              scalar=qkn[:, gcol:gcol + 1],
                                                              in1=r2[0:64, c32[0]:c32[1]], op0=ALU.mult, op1=ALU.mult),
                     reads=[("sg", si), ("sg", 2 + si), ("qkn",)], writes=dst32_regs, name="hnorm32")

        def head_pipeline(jobs):
            st.hp_panels = {}
            prev = None
            for args, kwargs in jobs:
                ctx = head_A(*args, **kwargs)
                if prev is not None:
                    head_B(prev)
                prev = ctx
            if prev is not None:
                head_B(prev)

        def att_A(Nq, qcols, qstride_heads, kv, keysrc, ocols, masked_kc=(), sample_b=None):
            li = rr("lt", 2)
            lps = ps[li]
            lregs = [("ps", li * 2), ("ps", li * 2 + 1)]
            N = 4 * Nq
            for kc, (k_ap, v_ap, M, kregs) in enumerate(keysrc):
                out = lps[0:M, kc * 256:kc * 256 + N].rearrange("p (g q) -> p g q", g=4)
                mm(out, k_ap, qcols, True, True, reads=kregs + [("act", kv * 4 + g) for g in range(4)], writes=lregs,
                   name="qk")
            lbuf = lt[li]
            pbuf = pT[li]
            groups = []
            for kc, (k_ap, v_ap, M, kregs) in enumerate(keysrc):
                if groups and groups[-1][2] == M and (kc in masked_kc) == groups[-1][3]:
                    groups[-1][1] = kc + 1
                else:
                    groups.append([kc, kc + 1, M, kc in masked_kc])
            for (k0, k1, M, msk) in groups:
                src = bass.AP(lps, k0 * 256, [[1024, M], [256, k1 - k0], [1, N]])
                if sample_b is not None and M == 32:
                    bsrc = biasS[kv][0:32, sample_b:sample_b + 1, :, :]
                    breg = ("biasS", kv)
                else:
                    bsrc = biasT[kv][0:M, k0:k1, :, 0:Nq]
                    breg = ("biasT", kv)
                dst = lbuf[0:M, k0:k1, 0:N]
                dst4 = dst.rearrange("p k (g q) -> p k g q", g=4)
                src4 = src.rearrange("p k (g q) -> p k g q", g=4)
                P.op("dve", lambda e, dst4=dst4, src4=src4, bsrc=bsrc: e.scalar_tensor_tensor(
                    out=dst4, in0=src4, scalar=0.125, in1=bsrc, op0=ALU.mult, op1=ALU.add),
                    reads=lregs + [breg], writes=[("lt", li)], name="lbias")
                pd = pbuf[0:M, k0:k1, 0:N]
                if msk:
                    P.op("act", lambda e, pd=pd, dst=dst, M=M: e.activation(out=pd, in_=dst, func=AF.Exp,
                                                                         bias=hmask[0:M, 0:1]),
                         reads=[("lt", li), ("hmask",)], writes=[("pT", li)], name="exp")
                else:
                    P.op("act", lambda e, pd=pd, dst=dst: e.activation(out=pd, in_=dst, func=AF.Exp),
                         reads=[("lt", li)], writes=[("pT", li)], name="exp")
            return (Nq, kv, keysrc, ocols, li)

        def att_B(ctx):
            Nq, kv, keysrc, ocols, li = ctx
            N = 4 * Nq
            pbuf = pT[li]
            oi = 4 + rr("ob", 2)
            ob = bank(oi)
            nk = len(keysrc)
            for kc, (k_ap, v_ap, M, kregs) in enumerate(keysrc):
                mm(ob[0:64, 0:N], v_ap, pbuf[0:M, kc, 0:N], kc == 0, kc == nk - 1,
                   reads=kregs + [("pT", li)], writes=[("ps", oi)], name="pv")
            for kc, (k_ap, v_ap, M, kregs) in enumerate(keysrc):
                mm(ob[0:64, 256:256 + N], ones_bf[0:M, 0:64], pbuf[0:M, kc, 0:N], kc == 0, False,
                   reads=[("pT", li), ("ones",)], writes=[("ps", oi)], name="den")
            srow = sinkrow[0:1, kv * 4:(kv + 1) * 4, 0:Nq]
            mm(ob[0:64, 256:256 + N].rearrange("p (g q) -> p g q", g=4), ones_bf[0:1, 0:64], srow, False, True,
               reads=[("sinkrow",), ("ones",)], writes=[("ps", oi)], name="densink")
            ri = rr("rec", 2)
            rb = rec[ri]
            P.op("act", lambda e: e.activation(out=rb[:, 0:N], in_=ob[0:64, 256:256 + N], func=AF.Ln),
                 reads=[("ps", oi)], writes=[("rec", ri)], name="recip1")
            P.op("act", lambda e: e.activation(out=rb[:, 0:N], in_=rb[:, 0:N], func=AF.Exp, scale=-1.0),
                 reads=[("rec", ri)], writes=[("rec", ri)], name="recip2")
            o_ap, o_regs = ocols
            P.op("dve", lambda e: e.tensor_tensor(
                out=o_ap, in0=ob[0:64, 0:N].rearrange("p (g q) -> p g q", g=4),
                in1=rb[:, 0:N].rearrange("p (g q) -> p g q", g=4), op=ALU.mult),
                reads=[("ps", oi), ("rec", ri)], writes=o_regs, name="onorm")

        def attention_pipeline(blocks):
            prev = None
            for args, kwargs in blocks:
                ctx = att_A(*args, **kwargs)
                if prev is not None:
                    att_B(prev)
                prev = ctx
            if prev is not None:
                att_B(prev)

        def mixer(tl):
            Tp, T = tl.Tp, tl.T
            nch = Tp // 64
            rmsnorm(T, 1)
            jobs = []
            for kv in range(NKV):
                dst = kT[:, kv, HALO:HALO + T]
                regs = [("kT", 2 + c) for c in range((T + 63) // 64)]
                if tl.last:
                    jobs.append(((T, O_K, kv * 64, 1, dst, regs),
                                 dict(dst32=kT32[:, kv, 0:160], dst32_regs=[("kT32",)], c32=(Tp - 128, Tp + 32))))
                else:
                    jobs.append(((T, O_K, kv * 64, 1, dst, regs), {}))
            head_pipeline(jobs)
            if tl.last:
                dma("sp", kp_o, kT32[:, :, 0:128], reads=[("kT32",)], out=True, name="kp_out")
                dma("sp", ks_o, kT32[:, :, 128:160], reads=[("kT32",)], out=True, name="ks_out")
            sl = load_panel(w_in, 0, O_V)
            vblocks = [(c * 64, 64, 2 + c, (c - (nch - 2)) if (tl.last and c >= nch - 2) else None) for c in range(nch)]
            if tl.sample:
                vblocks.append((Tp, 32, 2 + nch, 2))
            for (t0, M, slot, s32) in vblocks:
                bi = 2 + rr("vb", 2)
                b = bank(bi)
                for k in range(KC):
                    mm(b[0:M, 0:256], n[:, k, t0:t0 + M], wslots[sl][:, k, 0:256], k == 0, k == KC - 1,
                       reads=[("w", sl), ("n", k)], writes=[("ps", bi)], name="mmv")
                if s32 is None:
                    P.op("act", lambda e, b=b, M=M, slot=slot: e.activation(out=V[0:M, slot, :], in_=b[0:M, 0:256], func=AF.Copy),
                         reads=[("ps", bi)], writes=[("V", slot)], name="vcp")
                else:
                    P.op("act", lambda e, b=b, M=M, s32=s32: e.activation(out=V32[0:M, s32, :], in_=b[0:M, 0:256], func=AF.Copy),
                         reads=[("ps", bi)], writes=[("V32", s32)], name="v32")
                    P.op("pool", lambda e, M=M, slot=slot, s32=s32: e.tensor_copy(out=V[0:M, slot, :], in_=V32[0:M, s32, :]),
                         reads=[("V32", s32)], writes=[("V", slot)], name="vcp")
            if tl.last:
                dma("sp", vp_o, V32[:, 0:2, :], reads=[("V32", 0), ("V32", 1)], out=True, name="vp_out")
                dma("sp", vs_o, V32[0:32, 2, :], reads=[("V32", 2)], out=True, name="vs_out")
            jobs = []
            for qp in range(4):
                for hh in range(4):
                    hd = qp * 4 + hh
                    jobs.append(((T, O_Q + qp * 256, hh * 64, 0, act[0:64, hd, 0:T], [("act", hd)]), {}))
            head_pipeline(jobs)
            ubase = 2 + Tp
            for jp in range(4):
                s_cc = load_panel(w_in, 0, O_CC + jp * 256)
                s_cv = load_panel(w_in, 0, O_CV + jp * 256)
                s_cb = load_panel(w_in, 0, O_CB + jp * 256)
                for sub in range(2):
                    j = jp * 2 + sub
                    base = rr("cvb", 2) * 3
                    bA, bB, bC = bank(base), bank(base + 1), bank(base + 2)
                    cs = slice(sub * 128, (sub + 1) * 128)
                    for k in range(KC):
                        mm(bA[:, 0:T], wslots[s_cc][:, k, cs], n[:, k, 0:T], k == 0, k == KC - 1,
                           reads=[("w", s_cc), ("n", k)], writes=[("ps", base)], name="mmcc")
                    for k in range(KC):
                        mm(bB[:, 0:T], wslots[s_cv][:, k, cs], n[:, k, 0:T], k == 0, k == KC - 1,
                           reads=[("w", s_cv), ("n", k)], writes=[("ps", base + 1)], name="mmcv")
                    for k in range(KC):
                        mm(bC[:, 0:T], wslots[s_cb][:, k, cs], n[:, k, 0:T], k == 0, k == KC - 1,
                           reads=[("w", s_cb), ("n", k)], writes=[("ps", base + 2)], name="mmcb")
                    ci = rr("ccs", 2)
                    ccs = sg[ci]
                    P.op("act", lambda e, ccs=ccs, bA=bA: e.activation(out=ccs[:, 0:T], in_=bA[:, 0:T], func=AF.Copy),
                         reads=[("ps", base)], writes=[("sg", ci)], name="cccp")
                    ui = rr("ub", 2)
                    u = ub[ui]
                    P.op("pool", lambda e, u=u, j=j: e.tensor_copy(out=u[:, 0:2], in_=uhist[:, j, :]),
                         reads=[("uhist", j)], writes=[("ub", ui)], name="uh_in")
                    P.op("dve", lambda e, u=u, ccs=ccs, bB=bB: e.tensor_tensor(
                        out=u[:, 2:2 + Tp], in0=ccs[:, 0:Tp], in1=bB[:, 0:Tp], op=ALU.mult),
                        reads=[("sg", ci), ("ps", base + 1)], writes=[("ub", ui)], name="u")
                    P.op("pool", lambda e, u=u, j=j: e.tensor_copy(out=uhist[:, j, :], in_=u[:, Tp:Tp + 2]),
                         reads=[("ub", ui)], writes=[("uhist", j)], name="uh_out")
                    segs = [(0, Tp, 0)]
                    if tl.sample:
                        for b in range(2):
                            ub0 = ubase + b * 18
                            P.op("pool", lambda e, u=u, b=b, j=j, ub0=ub0: e.tensor_copy(out=u[:, ub0:ub0 + 2],
                                                                                         in_=uhist_s[:, j, b, :]),
                                 reads=[("uhist_s",)], writes=[("ub", ui)], name="uh_in")
                            P.op("dve", lambda e, u=u, b=b, ccs=ccs, bB=bB, ub0=ub0: e.tensor_tensor(
                                out=u[:, ub0 + 2:ub0 + 18], in0=ccs[:, Tp + b * 16:Tp + b * 16 + 16],
                                in1=bB[:, Tp + b * 16:Tp + b * 16 + 16], op=ALU.mult),
                                reads=[("sg", ci), ("ps", base + 1)], writes=[("ub", ui)], name="u")
                            segs.append((ub0, 16, Tp + b * 16))
                        P.op("pool", lambda e, u=u, j=j: e.tensor_copy(
                            out=uhist_s[:, j, :, :], in_=bass.AP(u, ubase + 16, [[TMAX + 2, 128], [18, 2], [1, 2]])),
                            reads=[("ub", ui)], writes=[("uhist_s",)], name="uh_out")
                    t1 = sg[2 + ci]
                    for (u0, L, o0) in segs:
                        P.op("pool", lambda e, u=u, t1=t1, u0=u0, L=L, o0=o0, j=j: e.tensor_scalar_mul(
                            out=t1[:, o0:o0 + L], in0=u[:, u0 + 2:u0 + 2 + L], scalar1=convw[:, j, 2:3]),
                            reads=[("ub", ui), ("convw",)], writes=[("sg", 2 + ci)], name="tap2")
                        P.op("dve", lambda e, u=u, t1=t1, u0=u0, L=L, o0=o0, j=j: e.scalar_tensor_tensor(
                            out=t1[:, o0:o0 + L], in0=u[:, u0 + 1:u0 + 1 + L], scalar=convw[:, j, 1:2],
                            in1=t1[:, o0:o0 + L], op0=ALU.mult, op1=ALU.add),
                            reads=[("ub", ui), ("convw",), ("sg", 2 + ci)], writes=[("sg", 2 + ci)], name="tap1")
                        P.op("dve", lambda e, u=u, t1=t1, u0=u0, L=L, o0=o0, j=j: e.scalar_tensor_tensor(
                            out=t1[:, o0:o0 + L], in0=u[:, u0:u0 + L], scalar=convw[:, j, 0:1],
                            in1=t1[:, o0:o0 + L], op0=ALU.mult, op1=ALU.add),
                            reads=[("ub", ui), ("convw",), ("sg", 2 + ci)], writes=[("sg", 2 + ci)], name="tap0")
                    P.op("dve", lambda e, t1=t1, bC=bC, j=j: e.tensor_tensor(out=z[:, j, 0:T], in0=t1[:, 0:T],
                                                                             in1=bC[:, 0:T], op=ALU.mult),
                         reads=[("sg", 2 + ci), ("ps", base + 2)], writes=[("z", j)], name="z")
            if tl.last:
                dma("sp", convp_o, uhist[:], reads=[("uhist", j) for j in range(8)], out=True, name="convp_out")
                dma("sp", convs_o, uhist_s[:], reads=[("uhist_s",)], out=True, name="convs_out")
            blocks = []
            for c in range(tl.skip // 64, nch):
                for kv in range(NKV):
                    keysrc = []
                    for kc in range(3):
                        slot = c + kc
                        keysrc.append((kT[:, kv, slot * 64:(slot + 1) * 64], V[:, slot, kv * 64:(kv + 1) * 64], 64,
                                       [("kT", slot), ("V", slot)]))
                    qcols = act[0:64, kv * 4:(kv + 1) * 4, c * 64:(c + 1) * 64]
                    masked = tuple(kc for kc in range(3) if (tl.first and c + kc < 4))
                    blocks.append(((64, qcols, None, kv, keysrc,
                                    (act[0:64, 16 + kv * 4:16 + kv * 4 + 4, c * 64:(c + 1) * 64],
                                     [("act", 16 + kv * 4 + g) for g in range(4)])),
                                   dict(masked_kc=masked)))
            if tl.sample:
                sslot = 2 + nch
                for b in range(2):
                    for kv in range(NKV):
                        keysrc = []
                        for ch in range(2):
                            keysrc.append((kcT[:, b, kv, ch * 64:(ch + 1) * 64], vcs[:, b, ch, kv * 64:(kv + 1) * 64], 64,
                                           [("kcT",), ("vcs",)]))
                        keysrc.append((kT[:, kv, HALO + Tp:HALO + Tp + 32], V[0:32, sslot, kv * 64:(kv + 1) * 64], 32,
                                       [("kT", sslot), ("V", sslot)]))
                        c0 = Tp + b * 16
                        qcols = act[0:64, kv * 4:(kv + 1) * 4, c0:c0 + 16]
                        blocks.append(((16, qcols, None, kv, keysrc,
                                        (act[0:64, 16 + kv * 4:16 + kv * 4 + 4, c0:c0 + 16],
                                         [("act", 16 + kv * 4 + g) for g in range(4)])), dict(sample_b=b)))
            attention_pipeline(blocks)
            if not tl.last:
                P.op("pool", lambda e: e.tensor_copy(out=kT[:, :, 0:HALO], in_=kT[:, :, Tp:Tp + HALO]),
                     reads=[("kT", nch), ("kT", nch + 1)], writes=[("kT", 0), ("kT", 1)], name="khist")
                P.op("pool", lambda e: e.tensor_copy(out=V[:, 0:2, :], in_=V[:, nch:nch + 2, :]),
                     reads=[("V", nch), ("V", nch + 1)], writes=[("V", 0), ("V", 1)], name="vhist")
            for ip in range(8):
                s_co = load_panel(w_conv_out, 0, ip * 256, nk=8)
                s_ao = load_panel(w_attn_o, 0, ip * 256, nk=16, pk=64)
                s_gc = load_panel(w_in, 0, O_GC + ip * 256)
                s_ga = load_panel(w_in, 0, O_GA + ip * 256)
                for sub in range(2):
                    i = ip * 2 + sub
                    base = rr("prj", 2) * 4
                    b_yc, b_ya, b_gc, b_ga = bank(base), bank(base + 1), bank(base + 2), bank(base + 3)
                    cs = slice(sub * 128, (sub + 1) * 128)
                    for k in range(8):
                        mm(b_yc[:, 0:T], wslots[s_co][:, k, cs], z[:, k, 0:T], k == 0, k == 7,
                           reads=[("w", s_co), ("z", k)], writes=[("ps", base)], name="mmco")
                    for k in range(16):
                        mm(b_ya[:, 0:T], wslots[s_ao][0:64, k, cs], act[0:64, 16 + k, 0:T], k == 0, k == 15,
                           reads=[("w", s_ao), ("act", 16 + k)], writes=[("ps", base + 1)], name="mmao")
                    for k in range(KC):
                        mm(b_gc[:, 0:T], wslots[s_gc][:, k, cs], n[:, k, 0:T], k == 0, k == KC - 1,
                           reads=[("w", s_gc), ("n", k)], writes=[("ps", base + 2)], name="mmgc")
                    for k in range(KC):
                        mm(b_ga[:, 0:T], wslots[s_ga][:, k, cs], n[:, k, 0:T], k == 0, k == KC - 1,
                           reads=[("w", s_ga), ("n", k)], writes=[("ps", base + 3)], name="mmga")
                    gi_ = rr("gt", 2)
                    ta, tb = sg[gi_ * 2], sg[gi_ * 2 + 1]
                    P.op("act", lambda e, ta=ta, b_gc=b_gc: e.activation(out=ta[:, 0:T], in_=b_gc[:, 0:T], func=AF.Sigmoid),
                         reads=[("ps", base + 2)], writes=[("sg", gi_ * 2)], name="sgc")
                    P.op("act", lambda e, tb=tb, b_ga=b_ga: e.activation(out=tb[:, 0:T], in_=b_ga[:, 0:T], func=AF.Sigmoid),
                         reads=[("ps", base + 3)], writes=[("sg", gi_ * 2 + 1)], name="sga")
                    P.op("dve", lambda e, ta=ta, b_yc=b_yc: e.tensor_tensor(out=ta[:, 0:T], in0=ta[:, 0:T], in1=b_yc[:, 0:T],
                                                                            op=ALU.mult),
                         reads=[("sg", gi_ * 2), ("ps", base)], writes=[("sg", gi_ * 2)], name="gyc")
                    P.op("dve", lambda e, tb=tb, b_ya=b_ya: e.tensor_tensor(out=tb[:, 0:T], in0=tb[:, 0:T], in1=b_ya[:, 0:T],
                                                                            op=ALU.mult),
                         reads=[("sg", gi_ * 2 + 1), ("ps", base + 1)], writes=[("sg", gi_ * 2 + 1)], name="gya")
                    P.op("pool", lambda e, ta=ta, tb=tb, i=i: e.tensor_tensor(out=act[:, i, 0:T], in0=ta[:, 0:T],
                                                                              in1=tb[:, 0:T], op=ALU.add),
                         reads=[("sg", gi_ * 2), ("sg", gi_ * 2 + 1)], writes=[("act", i)], name="mmix")
            for ip in range(8):
                s_o = load_panel(w_out, 0, ip * 256)
                for sub in range(2):
                    i = ip * 2 + sub
                    bi = rr("wo", 2)
                    b = bank(bi)
                    cs = slice(sub * 128, (sub + 1) * 128)
                    for k in range(KC):
                        mm(b[:, 0:T], wslots[s_o][:, k, cs], act[:, k, 0:T], k == 0, k == KC - 1,
                           reads=[("w", s_o), ("act", k)], writes=[("ps", bi)], name="mmo")
                    P.op("dve", lambda e, b=b, i=i: e.tensor_tensor(out=h[:, i, 0:T], in0=b[:, 0:T], in1=h[:, i, 0:T],
                                                                    op=ALU.add),
                         reads=[("ps", bi), ("h", i)], writes=[("h", i)], name="resid2")

        def ple(tl, out0):
            T = tl.T
            rmsnorm(T, 3)
            npr = tl.Tp - tl.skip
            src = peT[:, out0:out0 + npr].rearrange("(c p) t -> p c t", p=128)
            dma("pool", pe_bf[:, :, tl.skip:tl.Tp], src, writes=[("pe_bf",)], name="peload")
            if tl.sample:
                src = pesT[:, 0:SS].rearrange("(c p) t -> p c t", p=128)
                dma("pool", pe_bf[:, :, tl.Tp:tl.Tp + SS], src, writes=[("pe_bf",)], name="peload")
            for ip in range(8):
                s_g = load_panel(w_ple_gate, 0, ip * 256)
                s_p = load_panel(w_ple, 0, ip * 256, nk=2)
                for sub in range(2):
                    i = ip * 2 + sub
                    base = rr("pl", 2) * 2
                    bA, bB = bank(base), bank(base + 1)
                    cs = slice(sub * 128, (sub + 1) * 128)
                    for k in range(KC):
                        mm(bA[:, 0:T], wslots[s_g][:, k, cs], n[:, k, 0:T], k == 0, k == KC - 1,
                           reads=[("w", s_g), ("n", k)], writes=[("ps", base)], name="mmpg")
                    for k in range(2):
                        mm(bB[:, 0:T], wslots[s_p][:, k, cs], pe_bf[:, k, 0:T], k == 0, k == 1,
                           reads=[("w", s_p), ("pe_bf",)], writes=[("ps", base + 1)], name="mmpp")
                    gi_ = rr("pg", 2)
                    ta = sg[gi_]
                    P.op("act", lambda e, ta=ta, bA=bA: e.activation(out=ta[:, 0:T], in_=bA[:, 0:T], func=AF.Sigmoid),
                         reads=[("ps", base)], writes=[("sg", gi_)], name="spg")
                    P.op("dve", lambda e, ta=ta, bB=bB: e.tensor_tensor(out=ta[:, 0:T], in0=ta[:, 0:T], in1=bB[:, 0:T],
                                                                        op=ALU.mult),
                         reads=[("sg", gi_), ("ps", base + 1)], writes=[("sg", gi_)], name="pmul")
                    P.op("pool", lambda e, ta=ta, i=i: e.tensor_tensor(out=h[:, i, 0:T], in0=ta[:, 0:T], in1=h[:, i, 0:T],
                                                                       op=ALU.add),
                         reads=[("sg", gi_), ("h", i)], writes=[("h", i)], name="resid3")

        const_setup()
        bias_sample_setup()
        plan = [(512, 128, False), (512, 0, False), (384, 0, False), (384, 0, False), (384, 0, True)][:ntiles]
        seq0 = 0
        out0 = 0
        for ti, (Tp, skip, smp) in enumerate(plan):
            tl = K()
            tl.Tp, tl.skip, tl.sample = Tp, skip, smp
            tl.T = Tp + (SS if smp else 0)
            tl.first = (ti == 0)
            st.tile_idx = ti
            tl.last = (ti == 4)
            T = tl.T
            load_x(xT, seq0, Tp, 0)
            if smp:
                load_x(xsT, 0, SS, Tp)
            ffn(T, 0, ffn1_wg, ffn1_wu, ffn1_wd)
            mixer(tl)
            ffn(T, 2, ffn2_wg, ffn2_wu, ffn2_wd)
            ple(tl, out0)
            store_y(yT, out0, Tp - skip, skip)
            if smp:
                store_y(ysT, 0, SS, Tp)
            seq0 += Tp
            out0 += Tp - skip
        P.emit()
    return nc


def prep_inputs(inp):
    f = np.float32
    xp = inp["x_prompt"]
    shared = {
        "rel_table": np.ascontiguousarray(inp["rel_table"], f),
        "onehot": onehot_const(),
        "norms": np.ascontiguousarray(np.stack([inp["ffn1_norm"][0], inp["mix_norm"][0], inp["ffn2_norm"][0],
                                                inp["ple_norm"][0]], 0).reshape(4, KC, 128).transpose(2, 0, 1), f),
        "convw": np.ascontiguousarray(inp["conv_w"][0].reshape(3, 8, 128).transpose(2, 1, 0), f),
        "qkn": np.ascontiguousarray(np.stack([inp["q_norm"][0], inp["k_norm"][0]], 1), f),
        "sink": np.ascontiguousarray(inp["attn_sink"][0].reshape(1, 16), f),
    }
    for nm in ("ffn1_wg", "ffn1_wu", "ffn1_wd", "w_in", "w_conv_out", "w_attn_o", "w_out", "ffn2_wg", "ffn2_wu",
               "ffn2_wd", "w_ple", "w_ple_gate"):
        shared[nm] = panelize(np.asarray(inp[nm][0], f), WGEOM[nm][2], WGEOM[nm][3])
    maps = []
    for c in range(NCORES):
        b, qd = c // 4, c % 4
        t0 = qd * NT * TP
        m = dict(shared)
        xT = np.zeros((D, HALO + NT * TP), f)
        xT[:, HALO:] = xp[b, t0:t0 + NT * TP].T
        if qd > 0:
            xT[:, :HALO] = xp[b, t0 - HALO:t0].T
        m["xT"] = xT
        m["hmask"] = np.full((64, 1), -1e30 if qd == 0 else 0.0, f)
        m["peT"] = np.ascontiguousarray(inp["p_prompt"][0, b, t0:t0 + NT * TP].T, f)
        sb_ = slice(2 * c, 2 * c + 2)
        m["xsT"] = np.ascontiguousarray(inp["x_sample"][sb_].reshape(SS, D).T, f)
        m["pesT"] = np.ascontiguousarray(inp["p_sample"][0, sb_].reshape(SS, DPLE).T, f)
        m["uprev_s"] = np.ascontiguousarray(inp["state_conv"][0, sb_].reshape(2, 2, 8, 128).transpose(3, 2, 0, 1), f)
        m["kcT"] = np.ascontiguousarray(inp["cache_k"][0, sb_].transpose(3, 0, 2, 1), f)
        m["vc"] = np.ascontiguousarray(inp["cache_v"][0, sb_].reshape(2, 2, 64, 256).transpose(2, 0, 1, 3), f)
        maps.append(m)
    return maps


def assemble(results):
    f = np.float32
    B, SEQ = 2, 8192
    y_prompt = np.zeros((B, SEQ, D), f)
    y_sample = np.zeros((16, 16, D), f)
    conv_prompt = np.zeros((1, B, 2, DC), f)
    k_prompt = np.zeros((1, B, 128, NKV, HD), f)
    v_prompt = np.zeros((1, B, 128, NKV, HD), f)
    conv_sample = np.zeros((1, 16, 2, DC), f)
    k_sample = np.zeros((1, 16, 16, NKV, HD), f)
    v_sample = np.zeros((1, 16, 16, NKV, HD), f)
    for c in range(NCORES):
        r = results[c]
        b, qd = c // 4, c % 4
        t0 = qd * NT * TP
        y_prompt[b, t0:t0 + NT * TP] = r["yT"].T
        y_sample[2 * c:2 * c + 2] = r["ysT"].T.reshape(2, 16, D)
        if qd == 3:
            conv_prompt[0, b] = r["convp"].transpose(2, 1, 0).reshape(2, DC)
            k_prompt[0, b] = r["kp"].transpose(2, 1, 0)
            v_prompt[0, b] = r["vp"].transpose(1, 0, 2).reshape(128, NKV, HD)
        conv_sample[0, 2 * c:2 * c + 2] = r["convs"].transpose(2, 3, 1, 0).reshape(2, 2, DC)
        k_sample[0, 2 * c:2 * c + 2] = r["ks"].reshape(HD, NKV, 2, 16).transpose(2, 3, 1, 0)
        v_sample[0, 2 * c:2 * c + 2] = r["vs"].reshape(2, 16, NKV, HD)
    return (y_prompt, y_sample, conv_prompt, k_prompt, v_prompt, conv_sample, k_sample, v_sample)


_NC_CACHE = {}


def kernel(**inputs):
    inp = {k: np.asarray(v) for k, v in inputs.items()}
    maps = prep_inputs(inp)
    if "nc" not in _NC_CACHE:
        _NC_CACHE["nc"] = build()
    nc = _NC_CACHE["nc"]
    res = run_bass_kernel_spmd(nc, maps, core_ids=list(range(NCORES)))
    return assemble(res.results)
```

```python
import numpy as np
from contextlib import ExitStack
import concourse.bass as bass
import concourse.mybir as mybir
from concourse.bass_utils import run_bass_kernel_spmd

F32 = mybir.dt.float32
BF16 = mybir.dt.bfloat16
AF = mybir.ActivationFunctionType
ALU = mybir.AluOpType

D = 2048
DFF = 4096
DC = 1024
NH = 16
NKV = 4
HD = 64
DPLE = 256
WIN_COLS = 8704
EPS = 1e-6
NCORES = 8
TP = 512
NT = 4
HALO = 128
SS = 32
KC = D // 128

O_CB, O_CC, O_CV, O_Q, O_K, O_V, O_GC, O_GA = 0, 1024, 2048, 3072, 4096, 4352, 4608, 6656


WGEOM = {
    "ffn1_wg": (D, DFF, 128, 16), "ffn1_wu": (D, DFF, 128, 16), "ffn1_wd": (DFF, D, 128, 16),
    "w_in": (D, WIN_COLS, 128, 16), "w_conv_out": (DC, D, 128, 8), "w_attn_o": (NH * HD, D, 64, 16),
    "w_out": (D, D, 128, 16), "ffn2_wg": (D, DFF, 128, 16), "ffn2_wu": (D, DFF, 128, 16),
    "ffn2_wd": (DFF, D, 128, 16), "w_ple": (DPLE, D, 128, 2), "w_ple_gate": (D, D, 128, 16),
}


def panelize(w, pk, nk):
    R, C = w.shape
    rb, cb = R // (nk * pk), C // 256
    return np.ascontiguousarray(w.reshape(rb, nk, pk, cb, 256).transpose(0, 3, 2, 1, 4))


class Op:
    __slots__ = ("eng", "fn", "deps", "needs_inc", "sem", "val", "is_dma", "prev", "name")


class Prog:
    ENGS = ("pe", "act", "dve", "pool", "sp")
    NRING = 8

    def __init__(self, nc, es):
        self.nc = nc
        self.streams = {e: [] for e in self.ENGS}
        self.lastw = {}
        self.readers = {}
        self.esem = {e: es.enter_context(nc.semaphore("s_" + e)) for e in ("pe", "act", "dve", "pool")}
        self.ring = {q: [es.enter_context(nc.semaphore("d_%s_%d" % (q, i))) for i in range(self.NRING)]
                     for q in ("sp", "pool")}
        self.ring_n = {q: 0 for q in ("sp", "pool")}
        self.ring_cnt = {}
        self.out_ops = []

    def op(self, eng, fn, reads=(), writes=(), dma=False, name=""):
        o = Op()
        o.eng, o.fn, o.is_dma, o.needs_inc, o.name = eng, fn, dma, False, name
        o.sem = None
        o.val = 0
        o.prev = None
        deps = set()
        for r in reads:
            w = self.lastw.get(r)
            if w is not None:
                deps.add(w)
        for r in writes:
            w = self.lastw.get(r)
            if w is not None:
                deps.add(w)
            for x in self.readers.get(r, ()):
                deps.add(x)
        deps.discard(o)
        o.deps = deps
        for d in deps:
            if not (d.eng == "pe" and eng == "pe" and not d.is_dma):
                d.needs_inc = True
        for r in reads:
            self.readers.setdefault(r, []).append(o)
        for r in writes:
            self.lastw[r] = o
            self.readers[r] = []
        if dma:
            q = eng
            i = self.ring_n[q] % self.NRING
            self.ring_n[q] += 1
            sem = self.ring[q][i]
            k = self.ring_cnt.get((q, i), 0)
            self.ring_cnt[(q, i)] = k + 1
            o.sem = sem
            o.val = 16 * (k + 1)
            if k > 0:
                o.prev = (sem, 16 * k)
        self.streams[eng].append(o)
        return o

    def emit(self):
        nc = self.nc
        for e in ("pe", "act", "dve", "pool"):
            cnt = 0
            for o in self.streams[e]:
                if o.is_dma:
                    continue
                if o.needs_inc:
                    cnt += 1
                    o.sem = self.esem[e]
                    o.val = cnt
        fin = Op()
        fin.eng, fin.fn, fin.is_dma, fin.needs_inc, fin.name = "sp", None, False, False, "final"
        fin.deps = set(self.out_ops)
        lastd = {}
        for q in ("sp", "pool"):
            for o in self.streams[q]:
                if o.is_dma:
                    lastd[id(o.sem)] = o
        fin.deps.update(lastd.values())
        fin.prev = None
        self.streams["sp"].append(fin)

        def run_stream(e, h):
            seen = {}
            for o in self.streams[e]:
                need = {}
                for d in o.deps:
                    if d.eng == "pe" and e == "pe" and not d.is_dma:
                        continue
                    if d.sem is None:
                        raise RuntimeError("dep without sem: %s -> %s" % (d.name, o.name))
                    key = id(d.sem)
                    if key not in need or need[key][1] < d.val:
                        need[key] = (d.sem, d.val)
                if o.prev is not None:
                    key = id(o.prev[0])
                    if key not in need or need[key][1] < o.prev[1]:
                        need[key] = o.prev
                for key, (sem, val) in need.items():
                    if seen.get(key, 0) >= val:
                        continue
                    h.wait_ge(sem, val)
                    seen[key] = val
                if o.fn is None:
                    continue
                ins = o.fn(h)
                if o.is_dma:
                    ins.then_inc(o.sem, 16)
                elif o.needs_inc:
                    ins.then_inc(o.sem, 1)

        with nc.Block() as block:
            @block.tensor
            def _(h):
                run_stream("pe", h)

            @block.scalar
            def _(h):
                run_stream("act", h)

            @block.vector
            def _(h):
                run_stream("dve", h)

            @block.gpsimd
            def _(h):
                run_stream("pool", h)

            @block.sync
            def _(h):
                run_stream("sp", h)


class K:
    pass


def t5_bucket_np(rel):
    nb = 16
    max_exact = 8
    ret = np.where(rel > 0, nb, 0)
    nabs = np.abs(rel)
    nf = np.maximum(nabs, 1).astype(np.float32)
    large = max_exact + (np.log(nf / max_exact) / np.float32(np.log(128 / max_exact)) * (nb - max_exact)).astype(np.int32)
    large = np.minimum(large, nb - 1)
    return ret + np.where(nabs < max_exact, nabs, large)


def onehot_const():
    r = np.arange(255) - 63 - 128
    b = t5_bucket_np(r)
    oh = np.zeros((32, 256), np.float32)
    oh[b, np.arange(255)] = 1.0
    return oh


class _Stop(Exception):
    pass


def build(ntiles=5, do_sample=True, stop=None):
    nc = bass.Bass("TRN2", target_bir_lowering=False)
    es = ExitStack()
    with es:
        P = Prog(nc, es)

        def dram_in(name, shape, dt=F32):
            return nc.dram_tensor(name, list(shape), dt, kind="ExternalInput").ap()

        def dram_out(name, shape, dt=F32):
            return nc.dram_tensor(name, list(shape), dt, kind="ExternalOutput").ap()

        def sb(name, shape, dt):
            return es.enter_context(nc.sbuf_tensor("sb_" + name, list(shape), dt))

        xT = dram_in("xT", [D, HALO + NT * TP])
        xsT = dram_in("xsT", [D, SS])
        peT = dram_in("peT", [DPLE, NT * TP])
        pesT = dram_in("pesT", [DPLE, SS])
        uprev_s = dram_in("uprev_s", [128, 8, 2, 2])
        kcT_d = dram_in("kcT", [64, 2, NKV, 128])
        vc_d = dram_in("vc", [64, 2, 2, 256])
        rel_table = dram_in("rel_table", [32, 16])
        oh_d = dram_in("onehot", [32, 256])
        hmask_d = dram_in("hmask", [64, 1])
        norms_d = dram_in("norms", [128, 4, KC])
        convw_d = dram_in("convw", [128, 8, 3])
        qkn_d = dram_in("qkn", [64, 2])
        sink_d = dram_in("sink", [1, 16])
        ffn1_wg = dram_in("ffn1_wg", [WGEOM["ffn1_wg"][0] // (WGEOM["ffn1_wg"][2] * WGEOM["ffn1_wg"][3]), WGEOM["ffn1_wg"][1] // 256, WGEOM["ffn1_wg"][2], WGEOM["ffn1_wg"][3], 256])
        ffn1_wu = dram_in("ffn1_wu", [WGEOM["ffn1_wu"][0] // (WGEOM["ffn1_wu"][2] * WGEOM["ffn1_wu"][3]), WGEOM["ffn1_wu"][1] // 256, WGEOM["ffn1_wu"][2], WGEOM["ffn1_wu"][3], 256])
        ffn1_wd = dram_in("ffn1_wd", [WGEOM["ffn1_wd"][0] // (WGEOM["ffn1_wd"][2] * WGEOM["ffn1_wd"][3]), WGEOM["ffn1_wd"][1] // 256, WGEOM["ffn1_wd"][2], WGEOM["ffn1_wd"][3], 256])
        w_in = dram_in("w_in", [WGEOM["w_in"][0] // (WGEOM["w_in"][2] * WGEOM["w_in"][3]), WGEOM["w_in"][1] // 256, WGEOM["w_in"][2], WGEOM["w_in"][3], 256])
        w_conv_out = dram_in("w_conv_out", [WGEOM["w_conv_out"][0] // (WGEOM["w_conv_out"][2] * WGEOM["w_conv_out"][3]), WGEOM["w_conv_out"][1] // 256, WGEOM["w_conv_out"][2], WGEOM["w_conv_out"][3], 256])
        w_attn_o = dram_in("w_attn_o", [WGEOM["w_attn_o"][0] // (WGEOM["w_attn_o"][2] * WGEOM["w_attn_o"][3]), WGEOM["w_attn_o"][1] // 256, WGEOM["w_attn_o"][2], WGEOM["w_attn_o"][3], 256])
        w_out = dram_in("w_out", [WGEOM["w_out"][0] // (WGEOM["w_out"][2] * WGEOM["w_out"][3]), WGEOM["w_out"][1] // 256, WGEOM["w_out"][2], WGEOM["w_out"][3], 256])
        ffn2_wg = dram_in("ffn2_wg", [WGEOM["ffn2_wg"][0] // (WGEOM["ffn2_wg"][2] * WGEOM["ffn2_wg"][3]), WGEOM["ffn2_wg"][1] // 256, WGEOM["ffn2_wg"][2], WGEOM["ffn2_wg"][3], 256])
        ffn2_wu = dram_in("ffn2_wu", [WGEOM["ffn2_wu"][0] // (WGEOM["ffn2_wu"][2] * WGEOM["ffn2_wu"][3]), WGEOM["ffn2_wu"][1] // 256, WGEOM["ffn2_wu"][2], WGEOM["ffn2_wu"][3], 256])
        ffn2_wd = dram_in("ffn2_wd", [WGEOM["ffn2_wd"][0] // (WGEOM["ffn2_wd"][2] * WGEOM["ffn2_wd"][3]), WGEOM["ffn2_wd"][1] // 256, WGEOM["ffn2_wd"][2], WGEOM["ffn2_wd"][3], 256])
        w_ple = dram_in("w_ple", [WGEOM["w_ple"][0] // (WGEOM["w_ple"][2] * WGEOM["w_ple"][3]), WGEOM["w_ple"][1] // 256, WGEOM["w_ple"][2], WGEOM["w_ple"][3], 256])
        w_ple_gate = dram_in("w_ple_gate", [WGEOM["w_ple_gate"][0] // (WGEOM["w_ple_gate"][2] * WGEOM["w_ple_gate"][3]), WGEOM["w_ple_gate"][1] // 256, WGEOM["w_ple_gate"][2], WGEOM["w_ple_gate"][3], 256])

        SCR_TOTAL = 2 * 3 * D * DFF + D * WIN_COLS + 2 * DC * D + 2 * D * D + DPLE * D
        scr = nc.dram_tensor("wscr", [SCR_TOTAL], BF16, kind="ExternalOutput").ap()
        wnames = {}
        for _nm, _ap in (("ffn1_wg", ffn1_wg), ("ffn1_wu", ffn1_wu), ("ffn1_wd", ffn1_wd), ("w_in", w_in),
                         ("w_conv_out", w_conv_out), ("w_attn_o", w_attn_o), ("w_out", w_out), ("ffn2_wg", ffn2_wg),
                         ("ffn2_wu", ffn2_wu), ("ffn2_wd", ffn2_wd), ("w_ple", w_ple), ("w_ple_gate", w_ple_gate)):
            wnames[id(_ap)] = _nm
        yT = dram_out("yT", [D, NT * TP])
        ysT = dram_out("ysT", [D, SS])
        convp_o = dram_out("convp", [128, 8, 2])
        kp_o = dram_out("kp", [64, NKV, 128])
        vp_o = dram_out("vp", [64, 2, 256])
        convs_o = dram_out("convs", [128, 8, 2, 2])
        ks_o = dram_out("ks", [64, NKV, SS])
        vs_o = dram_out("vs", [32, 256])

        TMAX = TP
        h = sb("h", [128, KC, TMAX], F32)
        n = sb("n", [128, KC, TMAX], BF16)
        act = sb("act", [128, 32, TMAX], BF16)
        z = sb("z", [128, 8, TMAX], BF16)
        NSLOT = 4
        wslots = [sb("w%d" % i, [128, 16, 256], BF16) for i in range(NSLOT)]
        ones_bf = sb("ones_bf", [128, 128], BF16)
        norms = sb("norms", [128, 4, KC], F32)
        epsc = sb("epsc", [128, 1], F32)
        rstd = sb("rstd", [128, TMAX], F32)
        sq = [sb("sq%d" % i, [128, TMAX], BF16) for i in range(2)]
        sg = [sb("sg%d" % i, [128, TMAX], F32) for i in range(4)]
        ub = [sb("ub%d" % i, [128, TMAX + 2], F32) for i in range(2)]
        uhist = sb("uhist", [128, 8, 2], F32)
        uhist_s = sb("uhist_s", [128, 8, 2, 2], F32)
        convw = sb("convw", [128, 8, 3], F32)
        kT = sb("kT", [64, NKV, HALO + TMAX], BF16)
        kT32 = sb("kT32", [64, NKV, 160], F32)
        V = sb("V", [64, 10, 256], BF16)
        V32 = sb("V32", [64, 3, 256], F32)
        kcT = sb("kcT", [64, 2, NKV, 128], BF16)
        vcs = sb("vcs", [64, 2, 2, 256], BF16)
        biasT = [sb("biasT%d" % kv, [64, 3, 4, 64], F32) for kv in range(NKV)]
        biasS = [sb("biasS%d" % kv, [32, 2, 4, 16], F32) for kv in range(NKV)]
        oh = sb("oh", [32, 256], F32)
        tab = sb("tab", [32, 16], F32)
        hmask = sb("hmask", [64, 1], F32)
        qkn = sb("qkn", [64, 2], F32)
        sink = sb("sink", [1, 16], F32)
        sinkrow = sb("sinkrow", [1, 16, 64], BF16)
        lt = [sb("lt%d" % i, [64, 3, 256], F32) for i in range(2)]
        pT = [sb("pT%d" % i, [64, 3, 256], BF16) for i in range(2)]
        rec = [sb("rec%d" % i, [64, 256], F32) for i in range(2)]
        pe_bf = sb("pe_bf", [128, 2, TMAX], BF16)
        ps = [es.enter_context(nc.psum_tensor("ps%d" % i, [128, 1024], F32)) for i in range(4)]

        def bank(i):
            return ps[i // 2][:, (i % 2) * 512:(i % 2) * 512 + 512]

        st = K()
        st.wslot_n = 0
        st.scr_off = 0
        st.scr_cache = {}
        st.cache_on = True
        st.tile_idx = 0
        DEFER = ("w_conv_out", "w_attn_o", "w_out", "ffn2_wg", "ffn2_wu", "ffn2_wd", "w_ple", "w_ple_gate")
        st.rr = {}

        def rr(name, nmax):
            v = st.rr.get(name, 0)
            st.rr[name] = v + 1
            return v % nmax

        def load_panel(W, r0, c0, ncols=256, nk=16, pk=128):
            i = st.wslot_n % NSLOT
            st.wslot_n += 1
            dst = wslots[i][0:pk, 0:nk, 0:ncols]
            key = (wnames[id(W)], r0, c0, ncols, nk, pk)
            ne = pk * nk * ncols
            if key in st.scr_cache:
                off = st.scr_cache[key]
                src = scr[off:off + ne].rearrange("(p k n) -> p k n", p=pk, k=nk)
                P.op("sp", lambda e, dst=dst, src=src: e.dma_start(out=dst, in_=src),
                     reads=[("scr", key)], writes=[("w", i)], dma=True, name="wload2")
            else:
                assert ncols == 256 and c0 % 256 == 0 and r0 % (nk * pk) == 0 and WGEOM[key[0]][2:] == (pk, nk)
                src = W[r0 // (nk * pk), c0 // 256]
                P.op("pool", lambda e, dst=dst, src=src: e.dma_start(out=dst, in_=src),
                     reads=(), writes=[("w", i)], dma=True, name="wload")
                if st.cache_on and not (st.tile_idx == 0 and key[0] in DEFER):
                    off = st.scr_off
                    st.scr_off += ne
                    st.scr_cache[key] = off
                    sdst = scr[off:off + ne].rearrange("(p k n) -> p k n", p=pk, k=nk)
                    P.op("sp", lambda e, sdst=sdst, dst=dst: e.dma_start(out=sdst, in_=dst),
                         reads=[("w", i)], writes=[("scr", key)], dma=True, name="wsave")
            return i

        def dma(q, dst, src, reads=(), writes=(), out=False, name="dma"):
            o = P.op(q, lambda e, dst=dst, src=src: e.dma_start(out=dst, in_=src), reads=reads, writes=writes,
                     dma=True, name=name)
            if out:
                P.out_ops.append(o)
            return o

        def mm(out, lhsT, rhs, start, stop, reads, writes, name="mm"):
            P.op("pe", lambda e: e.matmul(out, lhsT, rhs, start=start, stop=stop), reads=reads, writes=writes,
                 name=name)

        def const_setup():
            P.op("dve", lambda e: e.memset(ones_bf[:], 1.0), writes=[("ones",)], name="ones")
            P.op("dve", lambda e: e.memset(epsc[:], EPS), writes=[("epsc",)], name="epsc")
            P.op("dve", lambda e: e.memset(uhist[:], 0.0), writes=[("uhist", j) for j in range(8)], name="uh0")
            P.op("dve", lambda e: e.memset(pe_bf[:], 0.0), writes=[("pe_bf",)], name="pe0")
            dma("sp", norms[:], norms_d, writes=[("norms",)])
            dma("sp", convw[:], convw_d, writes=[("convw",)])
            dma("sp", qkn[:], qkn_d, writes=[("qkn",)])
            dma("sp", sink[:], sink_d, writes=[("sink",)])
            dma("sp", oh[:], oh_d, writes=[("oh",)])
            dma("sp", tab[:], rel_table, writes=[("tab",)])
            dma("sp", hmask[:], hmask_d, writes=[("hmask",)])
            dma("sp", uhist_s[:], uprev_s, writes=[("uhist_s",)])
            dma("pool", kcT[:], kcT_d, writes=[("kcT",)])
            dma("pool", vcs[:], vc_d, writes=[("vcs",)])
            P.op("act", lambda e: e.activation(out=sink[:], in_=sink[:], func=AF.Exp),
                 reads=[("sink",)], writes=[("sink",)], name="expsink")
            P.op("dve", lambda e: e.tensor_copy(out=sinkrow[:], in_=bass.AP(sink, 0, [[16, 1], [1, 16], [0, 64]])),
                 reads=[("sink",)], writes=[("sinkrow",)], name="sinkrow")
            for kc in range(3):
                pt = ps[kc % 2]
                for q in range(64):
                    r0 = 63 - q + kc * 64
                    mm(pt[0:64, q * 16:(q + 1) * 16], oh[:, r0:r0 + 64], tab[:, :], True, True,
                       reads=[("oh",), ("tab",)], writes=[("ps", (kc % 2) * 2), ("ps", (kc % 2) * 2 + 1)], name="biasmm")
                for kv in range(NKV):
                    src = bass.AP(pt, kv * 4, [[1024, 64], [1, 4], [16, 64]])
                    P.op("dve", lambda e, src=src, kv=kv, kc=kc: e.tensor_copy(out=biasT[kv][:, kc, :, :], in_=src),
                         reads=[("ps", (kc % 2) * 2), ("ps", (kc % 2) * 2 + 1)], writes=[("biasT", kv)], name="biascp")

        def bias_sample_setup():
            for kv in range(NKV):
                P.op("dve", lambda e, kv=kv: e.memset(biasS[kv][:], -1e30), writes=[("biasS", kv)], name="bsms")
                for b in range(2):
                    dma("sp", biasS[kv][b * 16:(b + 1) * 16, b, :, :], biasT[kv][0:16, 2, :, 0:16],
                        reads=[("biasT", kv)], writes=[("biasS", kv)], name="bsdma")

        def load_x(src_ap, t0, T, c0):
            for q in range(4):
                src = src_ap[q * 512:(q + 1) * 512, t0:t0 + T].rearrange("(c p) t -> p c t", p=128)
                dst = h[:, q * 4:(q + 1) * 4, c0:c0 + T]
                dma("sp", dst, src, writes=[("h", c) for c in range(q * 4, q * 4 + 4)], name="xload")

        def store_y(dst_ap, t0, T, c0):
            for q in range(4):
                dst = dst_ap[q * 512:(q + 1) * 512, t0:t0 + T].rearrange("(c p) t -> p c t", p=128)
                src = h[:, q * 4:(q + 1) * 4, c0:c0 + T]
                dma("sp", dst, src, reads=[("h", c) for c in range(q * 4, q * 4 + 4)], out=True, name="ystore")

        def rmsnorm(T, gi):
            sb_ = bank(6)
            for c in range(KC):
                s = sq[c % 2]
                P.op("act", lambda e, s=s, c=c: e.activation(out=s[:, 0:T], in_=h[:, c, 0:T], func=AF.Square),
                     reads=[("h", c)], writes=[("sq", c % 2)], name="sq")
                mm(sb_[:, 0:T], ones_bf[:], s[:, 0:T], c == 0, c == KC - 1,
                   reads=[("sq", c % 2), ("ones",)], writes=[("ps", 6)], name="ssq")
            P.op("act", lambda e: e.activation(out=rstd[:, 0:T], in_=sb_[:, 0:T], func=AF.Ln, bias=epsc[:, 0:1],
                                               scale=1.0 / D),
                 reads=[("ps", 6), ("epsc",)], writes=[("rstd",)], name="rstd1")
            P.op("act", lambda e: e.activation(out=rstd[:, 0:T], in_=rstd[:, 0:T], func=AF.Exp, scale=-0.5),
                 reads=[("rstd",)], writes=[("rstd",)], name="rstd2")
            for c in range(KC):
                eng = "dve"
                P.op(eng, lambda e, c=c: e.scalar_tensor_tensor(out=n[:, c, 0:T], in0=h[:, c, 0:T],
                                                                scalar=norms[:, gi, c:c + 1], in1=rstd[:, 0:T],
                                                                op0=ALU.mult, op1=ALU.mult),
                     reads=[("h", c), ("rstd",), ("norms",)], writes=[("n", c)], name="nrm")

        def ffn(T, gi, Wg, Wu, Wd):
            rmsnorm(T, gi)
            NP = DFF // 256
            pend = [(load_panel(Wg, 0, 0), load_panel(Wu, 0, 0))]
            for p in range(NP):
                if p + 1 < NP:
                    pend.append((load_panel(Wg, 0, (p + 1) * 256), load_panel(Wu, 0, (p + 1) * 256)))
                sgi, sui = pend[p]
                if p == 0:
                    for k in range(KC):
                        for sub in range(2):
                            mm(bank(sub * 2)[:, 0:T], wslots[sgi][:, k, sub * 128:(sub + 1) * 128], n[:, k, 0:T],
                               k == 0, k == KC - 1, reads=[("w", sgi), ("n", k)], writes=[("ps", sub * 2)], name="mmg")
                            mm(bank(sub * 2 + 1)[:, 0:T], wslots[sui][:, k, sub * 128:(sub + 1) * 128], n[:, k, 0:T],
                               k == 0, k == KC - 1, reads=[("w", sui), ("n", k)], writes=[("ps", sub * 2 + 1)], name="mmu")
                for sub in range(2):
                    j = p * 2 + sub
                    ba_i, bb_i = (j % 2) * 2, (j % 2) * 2 + 1
                    bA, bB = bank(ba_i), bank(bb_i)
                    if p > 0:
                        for k in range(KC):
                            mm(bA[:, 0:T], wslots[sgi][:, k, sub * 128:(sub + 1) * 128], n[:, k, 0:T], k == 0, k == KC - 1,
                               reads=[("w", sgi), ("n", k)], writes=[("ps", ba_i)], name="mmg")
                        for k in range(KC):
                            mm(bB[:, 0:T], wslots[sui][:, k, sub * 128:(sub + 1) * 128], n[:, k, 0:T], k == 0, k == KC - 1,
                               reads=[("w", sui), ("n", k)], writes=[("ps", bb_i)], name="mmu")
                    s = sg[j % 2]
                    P.op("act", lambda e, s=s, bA=bA: e.activation(out=s[:, 0:T], in_=bA[:, 0:T], func=AF.Silu),
                         reads=[("ps", ba_i)], writes=[("sg", j % 2)], name="silu")
                    P.op("dve", lambda e, s=s, bB=bB, j=j: e.tensor_tensor(out=act[:, j, 0:T], in0=s[:, 0:T],
                                                                             in1=bB[:, 0:T], op=ALU.mult),
                         reads=[("sg", j % 2), ("ps", bb_i)], writes=[("act", j)], name="gu")
            NPD = D // 256
            pend = [(load_panel(Wd, 0, 0), load_panel(Wd, 2048, 0))]
            for p in range(NPD):
                if p + 1 < NPD:
                    pend.append((load_panel(Wd, 0, (p + 1) * 256), load_panel(Wd, 2048, (p + 1) * 256)))
                s0, s1 = pend[p]
                for sub in range(2):
                    i = p * 2 + sub
                    bi = 4 + (i % 2)
                    b = bank(bi)
                    for k in range(32):
                        sl = s0 if k < 16 else s1
                        mm(b[:, 0:T], wslots[sl][:, k % 16, sub * 128:(sub + 1) * 128], act[:, k, 0:T], k == 0, k == 31,
                           reads=[("w", sl), ("act", k)], writes=[("ps", bi)], name="mmd")
                    P.op("dve", lambda e, b=b, i=i: e.scalar_tensor_tensor(
                        out=h[:, i, 0:T], in0=b[:, 0:T], scalar=0.5, in1=h[:, i, 0:T], op0=ALU.mult, op1=ALU.add),
                        reads=[("ps", bi), ("h", i)], writes=[("h", i)], name="resid")

        def head_A(T, panel, col, gcol, dst_bf, dst_regs, dst32=None, dst32_regs=(), c32=None):
            if panel not in st.hp_panels:
                st.hp_panels[panel] = load_panel(w_in, 0, panel)
            slot = st.hp_panels[panel]
            bi = rr("hp", 2)
            b = bank(bi)
            for k in range(KC):
                mm(b[0:64, 0:T], wslots[slot][:, k, col:col + 64], n[:, k, 0:T], k == 0, k == KC - 1,
                   reads=[("w", slot), ("n", k)], writes=[("ps", bi)], name="mmh")
            si = rr("hps", 2)
            raw = sg[si]
            P.op("act", lambda e: e.activation(out=raw[0:64, 0:T], in_=b[0:64, 0:T], func=AF.Copy),
                 reads=[("ps", bi)], writes=[("sg", si)], name="hraw")
            s2 = sq[si]
            P.op("dve", lambda e: e.tensor_tensor(out=s2[0:64, 0:T], in0=raw[0:64, 0:T], in1=raw[0:64, 0:T], op=ALU.mult),
                 reads=[("sg", si)], writes=[("sq", si)], name="hsq")
            return (T, si, gcol, dst_bf, dst_regs, dst32, dst32_regs, c32)

        def head_B(ctx):
            T, si, gcol, dst_bf, dst_regs, dst32, dst32_regs, c32 = ctx
            raw, s2 = sg[si], sq[si]
            b2i = 6 + rr("hpb", 2)
            b2 = bank(b2i)
            mm(b2[0:64, 0:T], ones_bf[0:64, 0:64], s2[0:64, 0:T], True, True,
               reads=[("sq", si), ("ones",)], writes=[("ps", b2i)], name="hss")
            r2 = sg[2 + si]
            P.op("act", lambda e: e.activation(out=r2[0:64, 0:T], in_=b2[0:64, 0:T], func=AF.Ln, bias=epsc[0:64, 0:1],
                                               scale=1.0 / HD),
                 reads=[("ps", b2i), ("epsc",)], writes=[("sg", 2 + si)], name="hrs1")
            P.op("act", lambda e: e.activation(out=r2[0:64, 0:T], in_=r2[0:64, 0:T], func=AF.Exp, scale=-0.5),
                 reads=[("sg", 2 + si)], writes=[("sg", 2 + si)], name="hrs2")
            P.op("dve", lambda e: e.scalar_tensor_tensor(out=dst_bf, in0=raw[0:64, 0:T], scalar=qkn[:, gcol:gcol + 1],
                                                         in1=r2[0:64, 0:T], op0=ALU.mult, op1=ALU.mult),
                 reads=[("sg", si), ("sg", 2 + si), ("qkn",)], writes=dst_regs, name="hnorm")
            if dst32 is not None:
                P.op("dve", lambda e: e.scalar_tensor_tensor(out=dst32, in0=raw[0:64, c32[0]:c32[1]],
                                                              scalar=qkn[:, gcol:gcol + 1],
                                                              in1=r2[0:64, c32[0]:c32[1]], op0=ALU.mult, op1=ALU.mult),
                     reads=[("sg", si), ("sg", 2 + si), ("qkn",)], writes=dst32_regs, name="hnorm32")

        def head_pipeline(jobs):
            st.hp_panels = {}
            prev = None
            for args, kwargs in jobs:
                ctx = head_A(*args, **kwargs)
                if prev is not None:
                    head_B(prev)
                prev = ctx
            if prev is not None:
                head_B(prev)

        def att_A(Nq, qcols, qstride_heads, kv, keysrc, ocols, masked_kc=(), sample_b=None):
            li = rr("lt", 2)
            lps = ps[li]
            lregs = [("ps", li * 2), ("ps", li * 2 + 1)]
            N = 4 * Nq
            for kc, (k_ap, v_ap, M, kregs) in enumerate(keysrc):
                out = lps[0:M, kc * 256:kc * 256 + N].rearrange("p (g q) -> p g q", g=4)
                mm(out, k_ap, qcols, True, True, reads=kregs + [("act", kv * 4 + g) for g in range(4)], writes=lregs,
                   name="qk")
            lbuf = lt[li]
            pbuf = pT[li]
            groups = []
            for kc, (k_ap, v_ap, M, kregs) in enumerate(keysrc):
                if groups and groups[-1][2] == M and (kc in masked_kc) == groups[-1][3]:
                    groups[-1][1] = kc + 1
                else:
                    groups.append([kc, kc + 1, M, kc in masked_kc])
            for (k0, k1, M, msk) in groups:
                src = bass.AP(lps, k0 * 256, [[1024, M], [256, k1 - k0], [1, N]])
                if sample_b is not None and M == 32:
                    bsrc = biasS[kv][0:32, sample_b:sample_b + 1, :, :]
                    breg = ("biasS", kv)
                else:
                    bsrc = biasT[kv][0:M, k0:k1, :, 0:Nq]
                    breg = ("biasT", kv)
                dst = lbuf[0:M, k0:k1, 0:N]
                dst4 = dst.rearrange("p k (g q) -> p k g q", g=4)
                src4 = src.rearrange("p k (g q) -> p k g q", g=4)
                P.op("dve", lambda e, dst4=dst4, src4=src4, bsrc=bsrc: e.scalar_tensor_tensor(
                    out=dst4, in0=src4, scalar=0.125, in1=bsrc, op0=ALU.mult, op1=ALU.add),
                    reads=lregs + [breg], writes=[("lt", li)], name="lbias")
                pd = pbuf[0:M, k0:k1, 0:N]
                if msk:
                    P.op("act", lambda e, pd=pd, dst=dst, M=M: e.activation(out=pd, in_=dst, func=AF.Exp,
                                                                         bias=hmask[0:M, 0:1]),
                         reads=[("lt", li), ("hmask",)], writes=[("pT", li)], name="exp")
                else:
                    P.op("act", lambda e, pd=pd, dst=dst: e.activation(out=pd, in_=dst, func=AF.Exp),
                         reads=[("lt", li)], writes=[("pT", li)], name="exp")
            return (Nq, kv, keysrc, ocols, li)

        def att_B(ctx):
            Nq, kv, keysrc, ocols, li = ctx
            N = 4 * Nq
            pbuf = pT[li]
            oi = 4 + rr("ob", 2)
            ob = bank(oi)
            nk = len(keysrc)
            for kc, (k_ap, v_ap, M, kregs) in enumerate(keysrc):
                mm(ob[0:64, 0:N], v_ap, pbuf[0:M, kc, 0:N], kc == 0, kc == nk - 1,
                   reads=kregs + [("pT", li)], writes=[("ps", oi)], name="pv")
            for kc, (k_ap, v_ap, M, kregs) in enumerate(keysrc):
                mm(ob[0:64, 256:256 + N], ones_bf[0:M, 0:64], pbuf[0:M, kc, 0:N], kc == 0, False,
                   reads=[("pT", li), ("ones",)], writes=[("ps", oi)], name="den")
            srow = sinkrow[0:1, kv * 4:(kv + 1) * 4, 0:Nq]
            mm(ob[0:64, 256:256 + N].rearrange("p (g q) -> p g q", g=4), ones_bf[0:1, 0:64], srow, False, True,
               reads=[("sinkrow",), ("ones",)], writes=[("ps", oi)], name="densink")
            ri = rr("rec", 2)
            rb = rec[ri]
            P.op("act", lambda e: e.activation(out=rb[:, 0:N], in_=ob[0:64, 256:256 + N], func=AF.Ln),
                 reads=[("ps", oi)], writes=[("rec", ri)], name="recip1")
            P.op("act", lambda e: e.activation(out=rb[:, 0:N], in_=rb[:, 0:N], func=AF.Exp, scale=-1.0),
                 reads=[("rec", ri)], writes=[("rec", ri)], name="recip2")
            o_ap, o_regs = ocols
            P.op("dve", lambda e: e.tensor_tensor(
                out=o_ap, in0=ob[0:64, 0:N].rearrange("p (g q) -> p g q", g=4),
                in1=rb[:, 0:N].rearrange("p (g q) -> p g q", g=4), op=ALU.mult),
                reads=[("ps", oi), ("rec", ri)], writes=o_regs, name="onorm")

        def attention_pipeline(blocks):
            prev = None
            for args, kwargs in blocks:
                ctx = att_A(*args, **kwargs)
                if prev is not None:
                    att_B(prev)
                prev = ctx
            if prev is not None:
                att_B(prev)

        def mixer(tl):
            Tp, T = tl.Tp, tl.T
            nch = Tp // 64
            rmsnorm(T, 1)
            jobs = []
            for kv in range(NKV):
                dst = kT[:, kv, HALO:HALO + T]
                regs = [("kT", 2 + c) for c in range((T + 63) // 64)]
                if tl.last:
                    jobs.append(((T, O_K, kv * 64, 1, dst, regs),
                                 dict(dst32=kT32[:, kv, 0:160], dst32_regs=[("kT32",)], c32=(Tp - 128, Tp + 32))))
                else:
                    jobs.append(((T, O_K, kv * 64, 1, dst, regs), {}))
            head_pipeline(jobs)
            if tl.last:
                dma("sp", kp_o, kT32[:, :, 0:128], reads=[("kT32",)], out=True, name="kp_out")
                dma("sp", ks_o, kT32[:, :, 128:160], reads=[("kT32",)], out=True, name="ks_out")
            sl = load_panel(w_in, 0, O_V)
            vblocks = [(c * 64, 64, 2 + c, (c - (nch - 2)) if (tl.last and c >= nch - 2) else None) for c in range(nch)]
            if tl.sample:
                vblocks.append((Tp, 32, 2 + nch, 2))
            for (t0, M, slot, s32) in vblocks:
                bi = 2 + rr("vb", 2)
                b = bank(bi)
                for k in range(KC):
                    mm(b[0:M, 0:256], n[:, k, t0:t0 + M], wslots[sl][:, k, 0:256], k == 0, k == KC - 1,
                       reads=[("w", sl), ("n", k)], writes=[("ps", bi)], name="mmv")
                if s32 is None:
                    P.op("act", lambda e, b=b, M=M, slot=slot: e.activation(out=V[0:M, slot, :], in_=b[0:M, 0:256], func=AF.Copy),
                         reads=[("ps", bi)], writes=[("V", slot)], name="vcp")
                else:
                    P.op("act", lambda e, b=b, M=M, s32=s32: e.activation(out=V32[0:M, s32, :], in_=b[0:M, 0:256], func=AF.Copy),
                         reads=[("ps", bi)], writes=[("V32", s32)], name="v32")
                    P.op("pool", lambda e, M=M, slot=slot, s32=s32: e.tensor_copy(out=V[0:M, slot, :], in_=V32[0:M, s32, :]),
                         reads=[("V32", s32)], writes=[("V", slot)], name="vcp")
            if tl.last:
                dma("sp", vp_o, V32[:, 0:2, :], reads=[("V32", 0), ("V32", 1)], out=True, name="vp_out")
                dma("sp", vs_o, V32[0:32, 2, :], reads=[("V32", 2)], out=True, name="vs_out")
            jobs = []
            for qp in range(4):
                for hh in range(4):
                    hd = qp * 4 + hh
                    jobs.append(((T, O_Q + qp * 256, hh * 64, 0, act[0:64, hd, 0:T], [("act", hd)]), {}))
            head_pipeline(jobs)
            ubase = 2 + Tp
            for jp in range(4):
                s_cc = load_panel(w_in, 0, O_CC + jp * 256)
                s_cv = load_panel(w_in, 0, O_CV + jp * 256)
                s_cb = load_panel(w_in, 0, O_CB + jp * 256)
                for sub in range(2):
                    j = jp * 2 + sub
                    base = rr("cvb", 2) * 3
                    bA, bB, bC = bank(base), bank(base + 1), bank(base + 2)
                    cs = slice(sub * 128, (sub + 1) * 128)
                    for k in range(KC):
                        mm(bA[:, 0:T], wslots[s_cc][:, k, cs], n[:, k, 0:T], k == 0, k == KC - 1,
                           reads=[("w", s_cc), ("n", k)], writes=[("ps", base)], name="mmcc")
                    for k in range(KC):
                        mm(bB[:, 0:T], wslots[s_cv][:, k, cs], n[:, k, 0:T], k == 0, k == KC - 1,
                           reads=[("w", s_cv), ("n", k)], writes=[("ps", base + 1)], name="mmcv")
                    for k in range(KC):
                        mm(bC[:, 0:T], wslots[s_cb][:, k, cs], n[:, k, 0:T], k == 0, k == KC - 1,
                           reads=[("w", s_cb), ("n", k)], writes=[("ps", base + 2)], name="mmcb")
                    ci = rr("ccs", 2)
                    ccs = sg[ci]
                    P.op("act", lambda e, ccs=ccs, bA=bA: e.activation(out=ccs[:, 0:T], in_=bA[:, 0:T], func=AF.Copy),
                         reads=[("ps", base)], writes=[("sg", ci)], name="cccp")
                    ui = rr("ub", 2)
                    u = ub[ui]
                    P.op("pool", lambda e, u=u, j=j: e.tensor_copy(out=u[:, 0:2], in_=uhist[:, j, :]),
                         reads=[("uhist", j)], writes=[("ub", ui)], name="uh_in")
                    P.op("dve", lambda e, u=u, ccs=ccs, bB=bB: e.tensor_tensor(
                        out=u[:, 2:2 + Tp], in0=ccs[:, 0:Tp], in1=bB[:, 0:Tp], op=ALU.mult),
                        reads=[("sg", ci), ("ps", base + 1)], writes=[("ub", ui)], name="u")
                    P.op("pool", lambda e, u=u, j=j: e.tensor_copy(out=uhist[:, j, :], in_=u[:, Tp:Tp + 2]),
                         reads=[("ub", ui)], writes=[("uhist", j)], name="uh_out")
                    segs = [(0, Tp, 0)]
                    if tl.sample:
                        for b in range(2):
                            ub0 = ubase + b * 18
                            P.op("pool", lambda e, u=u, b=b, j=j, ub0=ub0: e.tensor_copy(out=u[:, ub0:ub0 + 2],
                                                                                         in_=uhist_s[:, j, b, :]),
                                 reads=[("uhist_s",)], writes=[("ub", ui)], name="uh_in")
                            P.op("dve", lambda e, u=u, b=b, ccs=ccs, bB=bB, ub0=ub0: e.tensor_tensor(
                                out=u[:, ub0 + 2:ub0 + 18], in0=ccs[:, Tp + b * 16:Tp + b * 16 + 16],
                                in1=bB[:, Tp + b * 16:Tp + b * 16 + 16], op=ALU.mult),
                                reads=[("sg", ci), ("ps", base + 1)], writes=[("ub", ui)], name="u")
                            segs.append((ub0, 16, Tp + b * 16))
                        P.op("pool", lambda e, u=u, j=j: e.tensor_copy(
                            out=uhist_s[:, j, :, :], in_=bass.AP(u, ubase + 16, [[TMAX + 2, 128], [18, 2], [1, 2]])),
                            reads=[("ub", ui)], writes=[("uhist_s",)], name="uh_out")
                    t1 = sg[2 + ci]
                    for (u0, L, o0) in segs:
                        P.op("pool", lambda e, u=u, t1=t1, u0=u0, L=L, o0=o0, j=j: e.tensor_scalar_mul(
                            out=t1[:, o0:o0 + L], in0=u[:, u0 + 2:u0 + 2 + L], scalar1=convw[:, j, 2:3]),
                            reads=[("ub", ui), ("convw",)], writes=[("sg", 2 + ci)], name="tap2")
                        P.op("dve", lambda e, u=u, t1=t1, u0=u0, L=L, o0=o0, j=j: e.scalar_tensor_tensor(
                            out=t1[:, o0:o0 + L], in0=u[:, u0 + 1:u0 + 1 + L], scalar=convw[:, j, 1:2],
                            in1=t1[:, o0:o0 + L], op0=ALU.mult, op1=ALU.add),
                            reads=[("ub", ui), ("convw",), ("sg", 2 + ci)], writes=[("sg", 2 + ci)], name="tap1")
                        P.op("dve", lambda e, u=u, t1=t1, u0=u0, L=L, o0=o0, j=j: e.scalar_tensor_tensor(
                            out=t1[:, o0:o0 + L], in0=u[:, u0:u0 + L], scalar=convw[:, j, 0:1],
                            in1=t1[:, o0:o0 + L], op0=ALU.mult, op1=ALU.add),
                            reads=[("ub", ui), ("convw",), ("sg", 2 + ci)], writes=[("sg", 2 + ci)], name="tap0")
                    P.op("dve", lambda e, t1=t1, bC=bC, j=j: e.tensor_tensor(out=z[:, j, 0:T], in0=t1[:, 0:T],
                                                                             in1=bC[:, 0:T], op=ALU.mult),
                         reads=[("sg", 2 + ci), ("ps", base + 2)], writes=[("z", j)], name="z")
            if tl.last:
                dma("sp", convp_o, uhist[:], reads=[("uhist", j) for j in range(8)], out=True, name="convp_out")
                dma("sp", convs_o, uhist_s[:], reads=[("uhist_s",)], out=True, name="convs_out")
            blocks = []
            for c in range(tl.skip // 64, nch):
                for kv in range(NKV):
                    keysrc = []
                    for kc in range(3):
                        slot = c + kc
                        keysrc.append((kT[:, kv, slot * 64:(slot + 1) * 64], V[:, slot, kv * 64:(kv + 1) * 64], 64,
                                       [("kT", slot), ("V", slot)]))
                    qcols = act[0:64, kv * 4:(kv + 1) * 4, c * 64:(c + 1) * 64]
                    masked = tuple(kc for kc in range(3) if (tl.first and c + kc < 4))
                    blocks.append(((64, qcols, None, kv, keysrc,
                                    (act[0:64, 16 + kv * 4:16 + kv * 4 + 4, c * 64:(c + 1) * 64],
                                     [("act", 16 + kv * 4 + g) for g in range(4)])),
                                   dict(masked_kc=masked)))
            if tl.sample:
                sslot = 2 + nch
                for b in range(2):
                    for kv in range(NKV):
                        keysrc = []
                        for ch in range(2):
                            keysrc.append((kcT[:, b, kv, ch * 64:(ch + 1) * 64], vcs[:, b, ch, kv * 64:(kv + 1) * 64], 64,
                                           [("kcT",), ("vcs",)]))
                        keysrc.append((kT[:, kv, HALO + Tp:HALO + Tp + 32], V[0:32, sslot, kv * 64:(kv + 1) * 64], 32,
                                       [("kT", sslot), ("V", sslot)]))
                        c0 = Tp + b * 16
                        qcols = act[0:64, kv * 4:(kv + 1) * 4, c0:c0 + 16]
                        blocks.append(((16, qcols, None, kv, keysrc,
                                        (act[0:64, 16 + kv * 4:16 + kv * 4 + 4, c0:c0 + 16],
                                         [("act", 16 + kv * 4 + g) for g in range(4)])), dict(sample_b=b)))
            attention_pipeline(blocks)
            if not tl.last:
                P.op("pool", lambda e: e.tensor_copy(out=kT[:, :, 0:HALO], in_=kT[:, :, Tp:Tp + HALO]),
                     reads=[("kT", nch), ("kT", nch + 1)], writes=[("kT", 0), ("kT", 1)], name="khist")
                P.op("pool", lambda e: e.tensor_copy(out=V[:, 0:2, :], in_=V[:, nch:nch + 2, :]),
                     reads=[("V", nch), ("V", nch + 1)], writes=[("V", 0), ("V", 1)], name="vhist")
            for ip in range(8):
                s_co = load_panel(w_conv_out, 0, ip * 256, nk=8)
                s_ao = load_panel(w_attn_o, 0, ip * 256, nk=16, pk=64)
                s_gc = load_panel(w_in, 0, O_GC + ip * 256)
                s_ga = load_panel(w_in, 0, O_GA + ip * 256)
                for sub in range(2):
                    i = ip * 2 + sub
                    base = rr("prj", 2) * 4
                    b_yc, b_ya, b_gc, b_ga = bank(base), bank(base + 1), bank(base + 2), bank(base + 3)
                    cs = slice(sub * 128, (sub + 1) * 128)
                    for k in range(8):
                        mm(b_yc[:, 0:T], wslots[s_co][:, k, cs], z[:, k, 0:T], k == 0, k == 7,
                           reads=[("w", s_co), ("z", k)], writes=[("ps", base)], name="mmco")
                    for k in range(16):
                        mm(b_ya[:, 0:T], wslots[s_ao][0:64, k, cs], act[0:64, 16 + k, 0:T], k == 0, k == 15,
                           reads=[("w", s_ao), ("act", 16 + k)], writes=[("ps", base + 1)], name="mmao")
                    for k in range(KC):
                        mm(b_gc[:, 0:T], wslots[s_gc][:, k, cs], n[:, k, 0:T], k == 0, k == KC - 1,
                           reads=[("w", s_gc), ("n", k)], writes=[("ps", base + 2)], name="mmgc")
                    for k in range(KC):
                        mm(b_ga[:, 0:T], wslots[s_ga][:, k, cs], n[:, k, 0:T], k == 0, k == KC - 1,
                           reads=[("w", s_ga), ("n", k)], writes=[("ps", base + 3)], name="mmga")
                    gi_ = rr("gt", 2)
                    ta, tb = sg[gi_ * 2], sg[gi_ * 2 + 1]
                    P.op("act", lambda e, ta=ta, b_gc=b_gc: e.activation(out=ta[:, 0:T], in_=b_gc[:, 0:T], func=AF.Sigmoid),
                         reads=[("ps", base + 2)], writes=[("sg", gi_ * 2)], name="sgc")
                    P.op("act", lambda e, tb=tb, b_ga=b_ga: e.activation(out=tb[:, 0:T], in_=b_ga[:, 0:T], func=AF.Sigmoid),
                         reads=[("ps", base + 3)], writes=[("sg", gi_ * 2 + 1)], name="sga")
                    P.op("dve", lambda e, ta=ta, b_yc=b_yc: e.tensor_tensor(out=ta[:, 0:T], in0=ta[:, 0:T], in1=b_yc[:, 0:T],
                                                                            op=ALU.mult),
                         reads=[("sg", gi_ * 2), ("ps", base)], writes=[("sg", gi_ * 2)], name="gyc")
                    P.op("dve", lambda e, tb=tb, b_ya=b_ya: e.tensor_tensor(out=tb[:, 0:T], in0=tb[:, 0:T], in1=b_ya[:, 0:T],
                                                                            op=ALU.mult),
                         reads=[("sg", gi_ * 2 + 1), ("ps", base + 1)], writes=[("sg", gi_ * 2 + 1)], name="gya")
                    P.op("pool", lambda e, ta=ta, tb=tb, i=i: e.tensor_tensor(out=act[:, i, 0:T], in0=ta[:, 0:T],
                                                                              in1=tb[:, 0:T], op=ALU.add),
                         reads=[("sg", gi_ * 2), ("sg", gi_ * 2 + 1)], writes=[("act", i)], name="mmix")
            for ip in range(8):
                s_o = load_panel(w_out, 0, ip * 256)
                for sub in range(2):
                    i = ip * 2 + sub
                    bi = rr("wo", 2)
                    b = bank(bi)
                    cs = slice(sub * 128, (sub + 1) * 128)
                    for k in range(KC):
                        mm(b[:, 0:T], wslots[s_o][:, k, cs], act[:, k, 0:T], k == 0, k == KC - 1,
                           reads=[("w", s_o), ("act", k)], writes=[("ps", bi)], name="mmo")
                    P.op("dve", lambda e, b=b, i=i: e.tensor_tensor(out=h[:, i, 0:T], in0=b[:, 0:T], in1=h[:, i, 0:T],
                                                                    op=ALU.add),
                         reads=[("ps", bi), ("h", i)], writes=[("h", i)], name="resid2")

        def ple(tl, out0):
            T = tl.T
            rmsnorm(T, 3)
            npr = tl.Tp - tl.skip
            src = peT[:, out0:out0 + npr].rearrange("(c p) t -> p c t", p=128)
            dma("pool", pe_bf[:, :, tl.skip:tl.Tp], src, writes=[("pe_bf",)], name="peload")
            if tl.sample:
                src = pesT[:, 0:SS].rearrange("(c p) t -> p c t", p=128)
                dma("pool", pe_bf[:, :, tl.Tp:tl.Tp + SS], src, writes=[("pe_bf",)], name="peload")
            for ip in range(8):
                s_g = load_panel(w_ple_gate, 0, ip * 256)
                s_p = load_panel(w_ple, 0, ip * 256, nk=2)
                for sub in range(2):
                    i = ip * 2 + sub
                    base = rr("pl", 2) * 2
                    bA, bB = bank(base), bank(base + 1)
                    cs = slice(sub * 128, (sub + 1) * 128)
                    for k in range(KC):
                        mm(bA[:, 0:T], wslots[s_g][:, k, cs], n[:, k, 0:T], k == 0, k == KC - 1,
                           reads=[("w", s_g), ("n", k)], writes=[("ps", base)], name="mmpg")
                    for k in range(2):
                        mm(bB[:, 0:T], wslots[s_p][:, k, cs], pe_bf[:, k, 0:T], k == 0, k == 1,
                           reads=[("w", s_p), ("pe_bf",)], writes=[("ps", base + 1)], name="mmpp")
                    gi_ = rr("pg", 2)
                    ta = sg[gi_]
                    P.op("act", lambda e, ta=ta, bA=bA: e.activation(out=ta[:, 0:T], in_=bA[:, 0:T], func=AF.Sigmoid),
                         reads=[("ps", base)], writes=[("sg", gi_)], name="spg")
                    P.op("dve", lambda e, ta=ta, bB=bB: e.tensor_tensor(out=ta[:, 0:T], in0=ta[:, 0:T], in1=bB[:, 0:T],
                                                                        op=ALU.mult),
                         reads=[("sg", gi_), ("ps", base + 1)], writes=[("sg", gi_)], name="pmul")
                    P.op("pool", lambda e, ta=ta, i=i: e.tensor_tensor(out=h[:, i, 0:T], in0=ta[:, 0:T], in1=h[:, i, 0:T],
                                                                       op=ALU.add),
                         reads=[("sg", gi_), ("h", i)], writes=[("h", i)], name="resid3")

        const_setup()
        bias_sample_setup()
        plan = [(512, 128, False), (512, 0, False), (384, 0, False), (384, 0, False), (384, 0, True)][:ntiles]
        seq0 = 0
        out0 = 0
        for ti, (Tp, skip, smp) in enumerate(plan):
            tl = K()
            tl.Tp, tl.skip, tl.sample = Tp, skip, smp
            tl.T = Tp + (SS if smp else 0)
            tl.first = (ti == 0)
            st.tile_idx = ti
            tl.last = (ti == 4)
            T = tl.T
            load_x(xT, seq0, Tp, 0)
            if smp:
                load_x(xsT, 0, SS, Tp)
            ffn(T, 0, ffn1_wg, ffn1_wu, ffn1_wd)
            mixer(tl)
            ffn(T, 2, ffn2_wg, ffn2_wu, ffn2_wd)
            ple(tl, out0)
            store_y(yT, out0, Tp - skip, skip)
            if smp:
                store_y(ysT, 0, SS, Tp)
            seq0 += Tp
            out0 += Tp - skip
        P.emit()
    return nc


def prep_inputs(inp):
    f = np.float32
    xp = inp["x_prompt"]
    shared = {
        "rel_table": np.ascontiguousarray(inp["rel_table"], f),
        "onehot": onehot_const(),
        "norms": np.ascontiguousarray(np.stack([inp["ffn1_norm"][0], inp["mix_norm"][0], inp["ffn2_norm"][0],
                                                inp["ple_norm"][0]], 0).reshape(4, KC, 128).transpose(2, 0, 1), f),
        "convw": np.ascontiguousarray(inp["conv_w"][0].reshape(3, 8, 128).transpose(2, 1, 0), f),
        "qkn": np.ascontiguousarray(np.stack([inp["q_norm"][0], inp["k_norm"][0]], 1), f),
        "sink": np.ascontiguousarray(inp["attn_sink"][0].reshape(1, 16), f),
    }
    for nm in ("ffn1_wg", "ffn1_wu", "ffn1_wd", "w_in", "w_conv_out", "w_attn_o", "w_out", "ffn2_wg", "ffn2_wu",
               "ffn2_wd", "w_ple", "w_ple_gate"):
        shared[nm] = panelize(np.asarray(inp[nm][0], f), WGEOM[nm][2], WGEOM[nm][3])
    maps = []
    for c in range(NCORES):
        b, qd = c // 4, c % 4
        t0 = qd * NT * TP
        m = dict(shared)
        xT = np.zeros((D, HALO + NT * TP), f)
        xT[:, HALO:] = xp[b, t0:t0 + NT * TP].T
        if qd > 0:
            xT[:, :HALO] = xp[b, t0 - HALO:t0].T
        m["xT"] = xT
        m["hmask"] = np.full((64, 1), -1e30 if qd == 0 else 0.0, f)
        m["peT"] = np.ascontiguousarray(inp["p_prompt"][0, b, t0:t0 + NT * TP].T, f)
        sb_ = slice(2 * c, 2 * c + 2)
        m["xsT"] = np.ascontiguousarray(inp["x_sample"][sb_].reshape(SS, D).T, f)
        m["pesT"] = np.ascontiguousarray(inp["p_sample"][0, sb_].reshape(SS, DPLE).T, f)
        m["uprev_s"] = np.ascontiguousarray(inp["state_conv"][0, sb_].reshape(2, 2, 8, 128).transpose(3, 2, 0, 1), f)
        m["kcT"] = np.ascontiguousarray(inp["cache_k"][0, sb_].transpose(3, 0, 2, 1), f)
        m["vc"] = np.ascontiguousarray(inp["cache_v"][0, sb_].reshape(2, 2, 64, 256).transpose(2, 0, 1, 3), f)
        maps.append(m)
    return maps


def assemble(results):
    f = np.float32
    B, SEQ = 2, 8192
    y_prompt = np.zeros((B, SEQ, D), f)
    y_sample = np.zeros((16, 16, D), f)
    conv_prompt = np.zeros((1, B, 2, DC), f)
    k_prompt = np.zeros((1, B, 128, NKV, HD), f)
    v_prompt = np.zeros((1, B, 128, NKV, HD), f)
    conv_sample = np.zeros((1, 16, 2, DC), f)
    k_sample = np.zeros((1, 16, 16, NKV, HD), f)
    v_sample = np.zeros((1, 16, 16, NKV, HD), f)
    for c in range(NCORES):
        r = results[c]
        b, qd = c // 4, c % 4
        t0 = qd * NT * TP
        y_prompt[b, t0:t0 + NT * TP] = r["yT"].T
        y_sample[2 * c:2 * c + 2] = r["ysT"].T.reshape(2, 16, D)
        if qd == 3:
            conv_prompt[0, b] = r["convp"].transpose(2, 1, 0).reshape(2, DC)
            k_prompt[0, b] = r["kp"].transpose(2, 1, 0)
            v_prompt[0, b] = r["vp"].transpose(1, 0, 2).reshape(128, NKV, HD)
        conv_sample[0, 2 * c:2 * c + 2] = r["convs"].transpose(2, 3, 1, 0).reshape(2, 2, DC)
        k_sample[0, 2 * c:2 * c + 2] = r["ks"].reshape(HD, NKV, 2, 16).transpose(2, 3, 1, 0)
        v_sample[0, 2 * c:2 * c + 2] = r["vs"].reshape(2, 16, NKV, HD)
    return (y_prompt, y_sample, conv_prompt, k_prompt, v_prompt, conv_sample, k_sample, v_sample)


_NC_CACHE = {}


def kernel(**inputs):
    inp = {k: np.asarray(v) for k, v in inputs.items()}
    maps = prep_inputs(inp)
    if "nc" not in _NC_CACHE:
        _NC_CACHE["nc"] = build()
    nc = _NC_CACHE["nc"]
    res = run_bass_kernel_spmd(nc, maps, core_ids=list(range(NCORES)))
    return assemble(res.results)
```
